# Optimizing a Trainium2 kernel written in Bass

```python
import jax, jax.numpy as jnp
from jax import lax
import numpy as np

D_MODEL = 1024
BATCH = 16
SEQ = 2048
DEPTH = 4

N_CONV_LAYERS = DEPTH // 2
N_MLA_LAYERS = DEPTH - N_CONV_LAYERS
CONV_WIDTH = 3
V_HEAD_DIM = 128
N_HEADS = D_MODEL // V_HEAD_DIM
QK_NOPE_DIM = 128
QK_ROPE_DIM = 64
KV_LORA_RANK = D_MODEL // 4
Q_LORA_RANK = 3 * D_MODEL // 8
D_FF = -(-8 * D_MODEL // (3 * 256)) * 256
ROPE_THETA = 10000.0
RMS_EPS = 1e-6
Q_BLOCK = 128

kernel_name = "yoco_shortconv_mla_hybrid"


def rms_norm(x, g):
    xf = x.astype(jnp.float32)
    y = xf * lax.rsqrt(jnp.mean(xf * xf, axis=-1, keepdims=True) + RMS_EPS)
    return (y * g.astype(jnp.float32)).astype(x.dtype)


def rope_tables(positions):
    inv_freq = ROPE_THETA ** (-jnp.arange(0, QK_ROPE_DIM, 2, dtype=jnp.float32) / QK_ROPE_DIM)
    ang = positions.astype(jnp.float32)[..., None] * inv_freq
    return jnp.cos(ang), jnp.sin(ang)


def apply_rope(x, cos, sin):
    cos = cos.astype(x.dtype)
    sin = sin.astype(x.dtype)
    x1, x2 = jnp.split(x, 2, axis=-1)
    return jnp.concatenate([x1 * cos - x2 * sin, x1 * sin + x2 * cos], axis=-1)


def short_conv_mixer(h, w_in, conv_w, w_out):
    s = h.shape[1]
    u = h @ w_in
    gate_b, gate_c, val = jnp.split(u, 3, axis=-1)
    z = gate_c * val
    zp = jnp.pad(z, ((0, 0), (CONV_WIDTH - 1, 0), (0, 0)))
    conv = sum(conv_w[k] * zp[:, k:k + s] for k in range(CONV_WIDTH))
    return (gate_b * conv) @ w_out


def swiglu_ffn(h, w13, w2):
    g, u = jnp.split(h @ w13, 2, axis=-1)
    return (jax.nn.silu(g) * u) @ w2


def shared_mla_kv(h, kv_norm_g, w_dkv, kv_latent_norm_g, w_ukv, cos, sin):
    b, s, _ = h.shape
    hn = rms_norm(h, kv_norm_g)
    ckv = hn @ w_dkv
    c_lat = rms_norm(ckv[..., :KV_LORA_RANK], kv_latent_norm_g)
    k_pe = apply_rope(ckv[..., KV_LORA_RANK:], cos, sin)
    kv = (c_lat @ w_ukv).reshape(b, s, N_HEADS, QK_NOPE_DIM + V_HEAD_DIM)
    k_nope, v = kv[..., :QK_NOPE_DIM], kv[..., QK_NOPE_DIM:]
    return k_nope, k_pe, v


def causal_block_attention(q_nope, q_pe, k_nope, k_pe, v):
    b, s = q_nope.shape[:2]
    nb = s // Q_BLOCK
    qn = q_nope.reshape(b, nb, Q_BLOCK, N_HEADS, QK_NOPE_DIM).transpose(1, 0, 2, 3, 4)
    qr = q_pe.reshape(b, nb, Q_BLOCK, N_HEADS, QK_ROPE_DIM).transpose(1, 0, 2, 3, 4)
    starts = jnp.arange(nb, dtype=jnp.int32) * Q_BLOCK
    k_idx = jnp.arange(s, dtype=jnp.int32)
    scale = (QK_NOPE_DIM + QK_ROPE_DIM) ** -0.5
    neg = jnp.finfo(jnp.float32).min

    def one_block(args):
        qn_b, qr_b, start = args
        sc = (jnp.einsum('bqhd,bkhd->bhqk', qn_b, k_nope)
              + jnp.einsum('bqhr,bkr->bhqk', qr_b, k_pe))
        sc = sc.astype(jnp.float32) * scale
        q_idx = start + jnp.arange(Q_BLOCK, dtype=jnp.int32)
        mask = k_idx[None, :] <= q_idx[:, None]
        p = jax.nn.softmax(jnp.where(mask, sc, neg), axis=-1).astype(v.dtype)
        return jnp.einsum('bhqk,bkhd->bqhd', p, v)

    out = lax.map(one_block, (qn, qr, starts))
    return out.transpose(1, 0, 2, 3, 4).reshape(b, s, N_HEADS, V_HEAD_DIM)


def mla_mixer(h, w_dq, q_norm_g, w_uq, w_o, k_nope, k_pe, v, cos_h, sin_h):
    b, s, _ = h.shape
    cq = rms_norm(h @ w_dq, q_norm_g)
    q = (cq @ w_uq).reshape(b, s, N_HEADS, QK_NOPE_DIM + QK_ROPE_DIM)
    q_nope = q[..., :QK_NOPE_DIM]
    q_pe = apply_rope(q[..., QK_NOPE_DIM:], cos_h, sin_h)
    attn = causal_block_attention(q_nope, q_pe, k_nope, k_pe, v)
    return attn.reshape(b, s, N_HEADS * V_HEAD_DIM) @ w_o


def setup_inputs(seed: int = 0) -> dict:
    key = jax.random.key(seed)
    ks = iter(jax.random.split(key, 32))
    f32 = jnp.float32
    res_scale = (2 * DEPTH) ** -0.5

    def w(shape, fan_in, extra=1.0):
        return jax.random.normal(next(ks), shape, f32) * (fan_in ** -0.5) * extra

    def gain(shape):
        return 1.0 + 0.02 * jax.random.normal(next(ks), shape, f32)

    na, nm = N_CONV_LAYERS, N_MLA_LAYERS
    x = jax.random.normal(next(ks), (BATCH, SEQ, D_MODEL), f32)
    positions = jnp.broadcast_to(jnp.arange(SEQ, dtype=jnp.int32), (BATCH, SEQ))
    return {
        "x": x,
        "positions": positions,
        "conv_norm_g": gain((na, D_MODEL)),
        "conv_w_in": w((na, D_MODEL, 3 * D_MODEL), D_MODEL),
        "conv_w": w((na, CONV_WIDTH, D_MODEL), CONV_WIDTH),
        "conv_w_out": w((na, D_MODEL, D_MODEL), D_MODEL, res_scale),
        "conv_ffn_norm_g": gain((na, D_MODEL)),
        "conv_ffn_w13": w((na, D_MODEL, 2 * D_FF), D_MODEL),
        "conv_ffn_w2": w((na, D_FF, D_MODEL), D_FF, res_scale),
        "kv_norm_g": gain((D_MODEL,)),
        "w_dkv": w((D_MODEL, KV_LORA_RANK + QK_ROPE_DIM), D_MODEL),
        "kv_latent_norm_g": gain((KV_LORA_RANK,)),
        "w_ukv": w((KV_LORA_RANK, N_HEADS * (QK_NOPE_DIM + V_HEAD_DIM)), KV_LORA_RANK),
        "mla_norm_g": gain((nm, D_MODEL)),
        "mla_w_dq": w((nm, D_MODEL, Q_LORA_RANK), D_MODEL),
        "mla_q_norm_g": gain((nm, Q_LORA_RANK)),
        "mla_w_uq": w((nm, Q_LORA_RANK, N_HEADS * (QK_NOPE_DIM + QK_ROPE_DIM)), Q_LORA_RANK),
        "mla_w_o": w((nm, N_HEADS * V_HEAD_DIM, D_MODEL), N_HEADS * V_HEAD_DIM, res_scale),
        "mla_ffn_norm_g": gain((nm, D_MODEL)),
        "mla_ffn_w13": w((nm, D_MODEL, 2 * D_FF), D_MODEL),
        "mla_ffn_w2": w((nm, D_FF, D_MODEL), D_FF, res_scale),
        "final_norm_g": gain((D_MODEL,)),
    }


def reference(x, positions,
              conv_norm_g, conv_w_in, conv_w, conv_w_out,
              conv_ffn_norm_g, conv_ffn_w13, conv_ffn_w2,
              kv_norm_g, w_dkv, kv_latent_norm_g, w_ukv,
              mla_norm_g, mla_w_dq, mla_q_norm_g, mla_w_uq, mla_w_o,
              mla_ffn_norm_g, mla_ffn_w13, mla_ffn_w2,
              final_norm_g):
    cos, sin = rope_tables(positions)
    cos_h, sin_h = cos[:, :, None, :], sin[:, :, None, :]
    h = x
    k_nope = k_pe = v = None
    for i in range(DEPTH):
        if i < N_CONV_LAYERS:
            h = h + short_conv_mixer(rms_norm(h, conv_norm_g[i]), conv_w_in[i], conv_w[i], conv_w_out[i])
            h = h + swiglu_ffn(rms_norm(h, conv_ffn_norm_g[i]), conv_ffn_w13[i], conv_ffn_w2[i])
        else:
            if i == N_CONV_LAYERS:
                k_nope, k_pe, v = shared_mla_kv(h, kv_norm_g, w_dkv, kv_latent_norm_g, w_ukv, cos, sin)
            j = i - N_CONV_LAYERS
            h = h + mla_mixer(rms_norm(h, mla_norm_g[j]), mla_w_dq[j], mla_q_norm_g[j], mla_w_uq[j],
                              mla_w_o[j], k_nope, k_pe, v, cos_h, sin_h)
            h = h + swiglu_ffn(rms_norm(h, mla_ffn_norm_g[j]), mla_ffn_w13[j], mla_ffn_w2[j])
    return rms_norm(h, final_norm_g)
```

```python
import math
from contextlib import ExitStack

import numpy as np
import concourse.bass as bass
import concourse.mybir as mybir
from concourse.bass_utils import run_bass_kernel_spmd

F32 = mybir.dt.float32
BF16 = mybir.dt.bfloat16
I32 = mybir.dt.int32
AF = mybir.ActivationFunctionType
ALU = mybir.AluOpType

D = 1024
S = 2048
NSEQ = 2
DFF = 2816
NFC = DFF // 128
NH = 8
KVL = 256
QL = 384
ROPE = 64
EPS = 1e-6
SCALE = (128 + 64) ** -0.5
R_SLOTS = 8
FG = 4
TWO_PI = float(2 * np.pi)

CV = {}
_n = 0
for _name, _w in [("conv_norm0", 8), ("conv_norm1", 8), ("cffn_norm0", 8), ("cffn_norm1", 8),
                  ("kv_norm", 8), ("mla_norm0", 8), ("mla_norm1", 8), ("mffn_norm0", 8),
                  ("mffn_norm1", 8), ("final_norm", 8), ("lat_norm", 2), ("q_norm0", 3),
                  ("q_norm1", 3), ("convw0", 24), ("convw1", 24), ("invf", 1), ("phc", 1),
                  ("phs", 1), ("eps", 1)]:
    CV[_name] = _n
    _n += _w
NCV = _n


class Region:
    __slots__ = ("name", "w", "r")

    def __init__(self, name):
        self.name = name
        self.w = None
        self.r = {}


class Sched:
    def __init__(self, nc, stack):
        self.nc = nc
        self.E = {"pe": nc.tensor, "act": nc.scalar, "dve": nc.vector, "pool": nc.gpsimd, "sp": nc.sync}
        self.csem = {e: stack.enter_context(nc.semaphore("c_" + e)) for e in ("pe", "act", "dve", "pool")}
        self.cnt = {e: 0 for e in self.csem}
        self.seen = {e: {} for e in self.E}
        self.dsem = {}
        self.stack = stack
        self.pe_unsignaled = False
        self.nins = 0

    def dma_sem(self, name):
        if name not in self.dsem:
            self.dsem[name] = [self.stack.enter_context(self.nc.semaphore("d_" + name)), 0]
        return self.dsem[name]

    def _wait(self, eng, key, val):
        if key in self.csem:
            if key == "pe" and val > self.cnt["pe"]:
                raise RuntimeError("wait on unsignaled PE event")
            sem = self.csem[key]
        else:
            sem, val = self.dsem[key]
        if self.seen[eng].get(key, 0) >= val:
            return
        self.E[eng].wait_ge(sem, val)
        self.nins += 1
        self.seen[eng][key] = val

    def _hazards(self, eng, reads, writes, is_dma):
        need = {}

        def add(key, val, raw):
            if key == eng and not is_dma:
                if eng == "pe" or not raw:
                    return
            if need.get(key, 0) < val:
                need[key] = val

        for r in reads:
            if r.w is not None:
                add(r.w[0], r.w[1], True)
        for w in writes:
            if w.w is not None:
                add(w.w[0], w.w[1], False)
            for k, v in w.r.items():
                add(k, v, False)
        for k, v in need.items():
            self._wait(eng, k, v)

    def op(self, eng, fn, reads=(), writes=(), signal=True):
        self._hazards(eng, reads, writes, False)
        ins = fn(self.E[eng])
        self.nins += 1
        if signal:
            ins.then_inc(self.csem[eng], 1)
            self.cnt[eng] += 1
            val = self.cnt[eng]
            if eng == "pe":
                self.pe_unsignaled = False
        else:
            assert eng == "pe"
            val = self.cnt[eng] + 1
            self.pe_unsignaled = True
        for r in reads:
            if r.r.get(eng, 0) < val:
                r.r[eng] = val
        for w in writes:
            w.w = (eng, val)
            w.r = {}

    def dma(self, qeng, semname, out_ap, in_ap, reads=(), writes=()):
        self._hazards(qeng, reads, writes, True)
        d = self.dma_sem(semname)
        self.E[qeng].dma_start(out=out_ap, in_=in_ap).then_inc(d[0], 16)
        self.nins += 1
        d[1] += 16
        for r in reads:
            r.r[semname] = d[1]
        for w in writes:
            w.w = (semname, d[1])
            w.r = {}

    def barrier(self):
        assert not self.pe_unsignaled
        for e in ("act", "dve", "pool", "sp"):
            for k in self.csem:
                if k != e:
                    self._wait(e, k, self.cnt[k])


def build_program(n_phases=None, n_seq=NSEQ):
    nc = bass.Bass("TRN2", target_bir_lowering=False)
    xT = nc.dram_tensor("xT", [NSEQ, D, S], F32, kind="ExternalInput").ap()
    pos = nc.dram_tensor("pos", [NSEQ, S], I32, kind="ExternalInput").ap()
    cvec_d = nc.dram_tensor("cvec", [128, NCV], F32, kind="ExternalInput").ap()
    wspecs = []
    plan = _weight_plan()
    NT = len(plan)
    NU = _dedupe(plan)
    wst = nc.dram_tensor("wst", [NU, 128, 1024], F32, kind="ExternalInput").ap()
    yT = nc.dram_tensor("yT", [NSEQ, D, S], F32, kind="ExternalOutput").ap()

    _uid = [0]

    def sbt(name, shape, dtype):
        _uid[0] += 1
        return nc.sbuf_tensor(f"{name}_{_uid[0]}", shape, dtype)

    with ExitStack() as stack:
        sc = Sched(nc, stack)
        ec = stack.enter_context
        hT = ec(nc.sbuf_tensor("hT", [128, 8, S], F32))
        ring = ec(nc.sbuf_tensor("ring", [128, R_SLOTS, 1024], BF16))
        tabC = ec(nc.sbuf_tensor("tabC", [64, S], F32))
        tabS = ec(nc.sbuf_tensor("tabS", [64, S], F32))
        clT = ec(nc.sbuf_tensor("clT", [128, 2, S], BF16))
        kpe = ec(nc.sbuf_tensor("kpe", [128, S], BF16))
        ident = ec(nc.sbuf_tensor("ident", [128, 128], BF16))
        mneg = ec(nc.sbuf_tensor("mneg", [128, 128], BF16))
        cv = ec(nc.sbuf_tensor("cv", [128, NCV], F32))
        ones = ec(nc.sbuf_tensor("ones", [128, 128], BF16))
        zc = ec(nc.sbuf_tensor("zc", [128, 8, 2], F32))
        TW = 256
        tposi = ec(nc.sbuf_tensor("tposi", [64, TW], I32))
        tang = ec(nc.sbuf_tensor("tang", [64, TW], F32))
        ta2 = ec(nc.sbuf_tensor("ta2", [64, TW], F32))
        ta2b = ec(nc.sbuf_tensor("ta2b", [64, TW], F32))
        tki = ec(nc.sbuf_tensor("tki", [64, TW], I32))
        tkf = ec(nc.sbuf_tensor("tkf", [64, TW], F32))
        rstdP = ec(nc.sbuf_tensor("rstdP", [128, 4, 512], F32))
        RSTDP = [Region(f"rstdP{t}") for t in range(4)]
        rstd_valid = [False] * 4
        T_R = {k: Region(k) for k in ("tposi", "tang", "ta2", "ta2b", "tki", "tkf")}
        PS = [ec(nc.psum_tensor(f"ps{i}", [128, 512], F32)) for i in range(8)]

        H = [[Region(f"H{c}_{t}") for t in range(4)] for c in range(8)]
        PSR = [Region(f"PS{i}") for i in range(8)]
        RING = [Region(f"ring{i}") for i in range(R_SLOTS)]
        R_tabC = [Region(f"tabC{t}") for t in range(4)]
        R_tabS = [Region(f"tabS{t}") for t in range(4)]
        CL = [[Region(f"cl{k}_{t}") for t in range(4)] for k in range(2)]
        KPE = [Region(f"kpe{t}") for t in range(4)]
        R_cv, R_ones = Region("cv"), Region("ones")
        ZC = [Region(f"zc{c}") for c in range(8)]

        wstate = {"issued": 0, "consumed": 0, "cap": 0}
        occupant = {}
        freed = set()

        def w_pump():
            while wstate["issued"] < wstate["cap"]:
                n = wstate["issued"]
                if n >= R_SLOTS and (n - R_SLOTS) not in freed:
                    break
                spec = plan[n % NT]
                L = spec["L"]
                slot = n % R_SLOTS
                sc.dma("pool", f"ring{slot}", ring[:, slot, 0:L], wst[spec["u"], :, 0:L], writes=[RING[slot]])
                wstate["issued"] += 1

        def w_next(kind):
            n = wstate["consumed"]
            spec = plan[n % NT]
            assert spec["kind"] == kind, (spec["kind"], kind, n)
            assert n < wstate["issued"], "weight tile consumed before its DMA was issued"
            wstate["consumed"] += 1
            occupant[n % R_SLOTS] = n
            return n % R_SLOTS

        def w_free(slot):
            freed.add(occupant[slot])
            w_pump()
            if pool_bg:
                pool_bg.pop(0)()

        sc.dma("sp", "misc", cv[:], cvec_d[:, :], writes=[R_cv])
        sc.op("dve", lambda e: e.memset(ones[:], 1.0), writes=[R_ones])
        R_id = Region("ident")
        sc.op("pool", lambda e: e.memset(kpe[:], 0.0), writes=KPE)
        sc.op("pool", lambda e: e.memset(ident[:], 0.0), writes=[R_id])
        sc.op("pool", lambda e: e.affine_select(out=ident[:], in_=ident[:], pattern=[[-1, 128]], compare_op=ALU.not_equal, fill=1.0, base=0, channel_multiplier=1),
              reads=[R_id], writes=[R_id])
        sc.op("pool", lambda e: e.memset(mneg[:], 0.0), writes=[R_id])
        sc.op("pool", lambda e: e.affine_select(out=mneg[:], in_=mneg[:], pattern=[[1, 128]], compare_op=ALU.is_ge, fill=-30000.0, base=0, channel_multiplier=-1),
              reads=[R_id], writes=[R_id])

        def cvc(name, j=0):
            c = CV[name] + j
            return cv[:, c:c + 1]

        def ln_exp_rstd(ps_ap, ps_reg, rstd_ap, rstd_reg, lnv_ap, lnv_reg, inv_d, P=128):
            sc.op("act", lambda e: e.activation(out=lnv_ap, in_=ps_ap, func=AF.Ln, bias=cv[0:P, CV["eps"]:CV["eps"] + 1], scale=inv_d),
                  reads=[ps_reg, R_cv], writes=[lnv_reg])
            sc.op("act", lambda e: e.activation(out=rstd_ap, in_=lnv_ap, func=AF.Exp, scale=-0.5),
                  reads=[lnv_reg], writes=[rstd_reg])

        def stats_from_h(tt, scr, bank):
            sq, SQ, lnv, LNV, rstd, RSTD = scr
            for c in range(8):
                q = c % 4
                sc.op("act", lambda e, c=c, q=q: e.activation(out=sq[:, q, :], in_=hT[:, c, tt * 512:(tt + 1) * 512], func=AF.Square),
                      reads=[H[c][tt]], writes=[SQ[q]])
                sc.op("pe", lambda e, c=c, q=q: e.matmul(PS[bank][:], ones[:], sq[:, q, :], start=(c == 0), stop=(c == 7)),
                      reads=[SQ[q], R_ones], writes=[PSR[bank]], signal=True)
            ln_exp_rstd(PS[bank][:], PSR[bank], rstdP[:, tt, :], RSTDP[tt], lnv[:, 0, :], LNV[0], 1.0 / D)
            rstd_valid[tt] = True

        def norm_h(gname, tts, xn, XN, scr):
            for tl, tt in enumerate(tts):
                if not rstd_valid[tt]:
                    stats_from_h(tt, scr, 6 + (tl % 2))
                for c in range(8):
                    sc.op("dve", lambda e, c=c: e.scalar_tensor_tensor(
                        out=xn[:, c, tl * 512:(tl + 1) * 512], in0=hT[:, c, tt * 512:(tt + 1) * 512],
                        scalar=cvc(gname, c), in1=rstdP[:, tt, :], op0=ALU.mult, op1=ALU.mult),
                        reads=[H[c][tt], RSTDP[tt], R_cv], writes=[XN[c][tl]])

        class PostStats:
            def __init__(self, scr, tts, banks=(0, 1), delay=2):
                self.scr = scr
                self.tts = tts
                self.banks = banks
                self.delay = delay
                self.q = []
                self.n = 0
                self.cnt = {tl: 0 for tl in range(len(tts))}

            def push(self, dc, tl):
                sq, SQ = self.scr[0], self.scr[1]
                tt = self.tts[tl]
                slot = self.n % 4
                self.n += 1
                sc.op("act", lambda e: e.activation(out=sq[:, slot, :], in_=hT[:, dc, tt * 512:(tt + 1) * 512], func=AF.Square),
                      reads=[H[dc][tt]], writes=[SQ[slot]])
                bank = self.banks[tl]
                k = self.cnt[tl]
                self.cnt[tl] += 1

                def mm():
                    sc.op("pe", lambda e: e.matmul(PS[bank][:], ones[:], sq[:, slot, :], start=(k == 0), stop=(k == 7)),
                          reads=[SQ[slot], R_ones], writes=[PSR[bank]], signal=True)
                self.q.append(mm)

            def tick(self):
                while len(self.q) > self.delay:
                    self.q.pop(0)()

            def finish(self):
                lnv, LNV = self.scr[2], self.scr[3]
                while self.q:
                    self.q.pop(0)()
                for tl, tt in enumerate(self.tts):
                    assert self.cnt[tl] == 8
                    b_ = self.banks[tl]
                    sc.op("act", lambda e, b_=b_, tl=tl: e.activation(out=lnv[:, tl % 2, :], in_=PS[b_][:], func=AF.Ln, bias=cv[:, CV["eps"]:CV["eps"] + 1], scale=1.0 / D),
                          reads=[PSR[b_], R_cv], writes=[LNV[tl % 2]])
                for tl, tt in enumerate(self.tts):
                    sc.op("act", lambda e, tl=tl, tt=tt: e.activation(out=rstdP[:, tt, :], in_=lnv[:, tl % 2, :], func=AF.Exp, scale=-0.5),
                          reads=[LNV[tl % 2]], writes=[RSTDP[tt]])
                    rstd_valid[tt] = True

        def norm_scratch(stack2):
            sq = stack2.enter_context(sbt("sq", [128, 4, 512], BF16))
            lnv = stack2.enter_context(sbt("lnv", [128, 2, 512], F32))
            rstd = stack2.enter_context(sbt("rstd", [128, 1, 512], F32))
            return (sq, [Region(f"sq{i}") for i in range(4)], lnv, [Region("lnv0"), Region("lnv1")],
                    rstd, [Region("rstd0")])

        def mm_group(bank, pairs, M=128, N=512, n0=0, reads_extra=()):
            n = len(pairs)
            for i, (l, r, regs) in enumerate(pairs):
                sc.op("pe", lambda e, l=l, r=r, i=i: e.matmul(PS[bank][0:M, n0:n0 + N], l, r, start=(i == 0), stop=(i == n - 1)),
                      reads=list(regs) + list(reads_extra), writes=[PSR[bank]], signal=(i == n - 1))

        xnbuf = [ec(nc.sbuf_tensor(f"xnbuf{b}", [128, 8, 1024], BF16)) for b in range(2)]
        XNR = [[[Region(f"xn{b}_{c}_{t}") for t in range(2)] for c in range(8)] for b in range(2)]
        pstate = {"k": 0, "pre": None, "pre_next": None}
        dve_bg = []

        def dve_tick(n=1):
            for _ in range(n):
                if dve_bg:
                    dve_bg.pop(0)()

        def get_xn(gname, tts, scr):
            b = pstate["k"] % 2
            xn, XN = xnbuf[b], XNR[b]
            while dve_bg:
                dve_bg.pop(0)()
            if pstate["pre"] != (gname, tuple(tts)):
                norm_h(gname, tts, xn, XN, scr)
            return xn, XN

        def schedule_prenorm(nxt):
            pstate["pre_next"] = None
            if nxt is None:
                return
            gname, tts = nxt
            if not all(rstd_valid[tt] for tt in tts):
                return
            b = (pstate["k"] + 1) % 2
            xn, XN = xnbuf[b], XNR[b]
            for tl, tt in enumerate(tts):
                for c in range(8):
                    dve_bg.append(lambda c=c, tl=tl, tt=tt: sc.op("dve", lambda e: e.scalar_tensor_tensor(
                        out=xn[:, c, tl * 512:(tl + 1) * 512], in0=hT[:, c, tt * 512:(tt + 1) * 512],
                        scalar=cvc(gname, c), in1=rstdP[:, tt, :], op0=ALU.mult, op1=ALU.mult),
                        reads=[H[c][tt], RSTDP[tt], R_cv], writes=[XN[c][tl]]))
            pstate["pre_next"] = (gname, tuple(tts))

        def phase_end():
            pstate["pre"] = pstate["pre_next"]
            pstate["k"] += 1
            sc.barrier()

        pool_bg = []

        def table_steps(s):
            Rp, Ra, Rki, Rkf = T_R["tposi"], T_R["tang"], T_R["tki"], T_R["tkf"]
            Ra2 = [T_R["ta2"], T_R["ta2b"]]
            a2s = [ta2, ta2b]
            steps = []
            pending_sin = []
            k = 0
            for piece in range(S // TW):
                tt = (piece * TW) // 512
                cs = slice(piece * TW, (piece + 1) * TW)

                def st_load(cs=cs):
                    sc.dma("sp", "misc", tposi[:], pos[s:s + 1, cs].partition_broadcast(64), writes=[Rp])
                    sc.op("dve", lambda e: e.tensor_copy(out=tang[:], in_=tposi[:]), reads=[Rp], writes=[Ra])
                steps.append(st_load)
                steps.append(lambda: sc.op("dve", lambda e: e.tensor_scalar(out=tang[:], in0=tang[:], scalar1=cv[0:64, CV["invf"]:CV["invf"] + 1], scalar2=None, op0=ALU.mult),
                                           reads=[Ra, R_cv], writes=[Ra]))
                for (phn, tab, Rt) in (("phc", tabC, R_tabC[tt]), ("phs", tabS, R_tabS[tt])):
                    a2 = a2s[k % 2]
                    R2 = Ra2[k % 2]
                    steps.append(lambda phn=phn, a2=a2, R2=R2: sc.op("dve", lambda e: e.tensor_scalar(out=a2[:], in0=tang[:], scalar1=cv[0:64, CV[phn]:CV[phn] + 1], scalar2=None, op0=ALU.add),
                                                                      reads=[Ra, R_cv], writes=[R2]))
                    steps.append(lambda a2=a2, R2=R2: sc.op("dve", lambda e: e.tensor_scalar(out=tki[:], in0=a2[:], scalar1=float(1.0 / TWO_PI), scalar2=None, op0=ALU.mult),
                                                             reads=[R2], writes=[Rki]))
                    steps.append(lambda: sc.op("dve", lambda e: e.tensor_copy(out=tkf[:], in_=tki[:]), reads=[Rki], writes=[Rkf]))
                    steps.append(lambda: sc.op("dve", lambda e: e.tensor_scalar(out=tkf[:], in0=tkf[:], scalar1=-TWO_PI, scalar2=None, op0=ALU.mult), reads=[Rkf], writes=[Rkf]))
                    steps.append(lambda a2=a2, R2=R2: sc.op("dve", lambda e: e.tensor_tensor(out=a2[:], in0=a2[:], in1=tkf[:], op=ALU.add), reads=[R2, Rkf], writes=[R2]))
                    steps.append(lambda a2=a2, R2=R2: sc.op("dve", lambda e: e.tensor_scalar(out=tkf[:], in0=a2[:], scalar1=float(np.pi), scalar2=-TWO_PI, op0=ALU.is_gt, op1=ALU.mult),
                                                             reads=[R2], writes=[Rkf]))
                    steps.append(lambda a2=a2, R2=R2: sc.op("dve", lambda e: e.tensor_tensor(out=a2[:], in0=a2[:], in1=tkf[:], op=ALU.add), reads=[R2, Rkf], writes=[R2]))
                    steps.append(lambda a2=a2, R2=R2: sc.op("dve", lambda e: e.tensor_scalar(out=a2[:], in0=a2[:], scalar1=float(-np.pi), scalar2=float(np.pi), op0=ALU.max, op1=ALU.min),
                                                             reads=[R2], writes=[R2]))
                    steps.extend(pending_sin)
                    pending_sin = [lambda a2=a2, R2=R2, tab=tab, cs=cs, Rt=Rt: sc.op("act", lambda e: e.activation(out=tab[:, cs], in_=a2[:], func=AF.Sin), reads=[R2], writes=[Rt])]
                    k += 1
            steps.extend(pending_sin)
            return steps

        def flush_pool_bg():
            while pool_bg:
                pool_bg.pop(0)()

        def phase_load(s, hf):
            rstd_valid[2 * hf] = rstd_valid[2 * hf + 1] = False
            for c in range(8):
                sc.dma("sp", "ldx", hT[:, c, hf * 1024:(hf + 1) * 1024], xT[s, c * 128:(c + 1) * 128, hf * 1024:(hf + 1) * 1024],
                       writes=[H[c][2 * hf], H[c][2 * hf + 1]])

        def phase_conv(l, hf, nxt=None):
            tts = [2 * hf, 2 * hf + 1]
            with ExitStack() as st2:
                e2 = st2.enter_context
                mT = e2(sbt("mT", [128, 8, 1024], BF16))
                zb = e2(sbt("zb", [128, 2, 2 + 1024], F32))
                csb = e2(sbt("csb", [128, 2, 512], F32))
                acc = e2(sbt("acc", [128, 2, 512], F32))
                cp0 = e2(sbt("cp0", [128, 2, 512], F32))
                cp1 = e2(sbt("cp1", [128, 2, 512], F32))
                CP0 = [Region("cp0_0"), Region("cp0_1")]
                CP1 = [Region("cp1_0"), Region("cp1_1")]
                MT = [[Region(f"mT{c}_{t}") for t in range(2)] for c in range(8)]
                ZB = [[Region(f"zbh{q}"), Region(f"zb{q}_0"), Region(f"zb{q}_1")] for q in range(2)]
                CSB = [Region("csb0"), Region("csb1")]
                ACC = [Region("acc0"), Region("acc1")]
                scr = norm_scratch(st2)
                xn, XN = get_xn(f"conv_norm{l}", tts, scr)
                schedule_prenorm(nxt)
                cw = CV[f"convw{l}"]
                it = 0
                for c in range(8):
                    sb, sg_, sv = w_next("win"), w_next("win"), w_next("win")
                    q = c % 2
                    if hf == 0:
                        sc.op("pool", lambda e, q=q: e.memset(zb[:, q, 0:2], 0.0), writes=[ZB[q][0]])
                    else:
                        sc.op("pool", lambda e, q=q, c=c: e.tensor_copy(out=zb[:, q, 0:2], in_=zc[:, c, :]), reads=[ZC[c]], writes=[ZB[q][0]])
                    for tl, tt in enumerate(tts):
                        b0 = 3 * (it % 2)
                        it += 1
                        for j, slot in enumerate((sb, sg_, sv)):
                            mm_group(b0 + j, [(ring[:, slot, k * 128:(k + 1) * 128], xn[:, k, tl * 512:(tl + 1) * 512], [RING[slot], XN[k][tl]]) for k in range(8)])
                        u = it % 2
                        sc.op("act", lambda e, u=u, b0=b0: e.activation(out=csb[:, u, :], in_=PS[b0 + 1][:], func=AF.Copy),
                              reads=[PSR[b0 + 1]], writes=[CSB[u]])
                        z0 = 2 + tl * 512
                        sc.op("dve", lambda e, u=u, b0=b0, q=q, z0=z0: e.tensor_tensor(out=zb[:, q, z0:z0 + 512], in0=PS[b0 + 2][:], in1=csb[:, u, :], op=ALU.mult),
                              reads=[PSR[b0 + 2], CSB[u]], writes=[ZB[q][1 + tl]])
                        zr = [ZB[q][tl], ZB[q][1 + tl]]
                        sc.op("act", lambda e, u=u, q=q, z0=z0, c=c: e.activation(out=cp1[:, u, :], in_=zb[:, q, z0 - 1:z0 + 511], func=AF.Identity, scale=cv[:, cw + 8 + c:cw + 9 + c]),
                              reads=zr + [R_cv], writes=[CP1[u]])
                        sc.op("act", lambda e, u=u, q=q, z0=z0, c=c: e.activation(out=cp0[:, u, :], in_=zb[:, q, z0 - 2:z0 + 510], func=AF.Identity, scale=cv[:, cw + c:cw + 1 + c]),
                              reads=zr + [R_cv], writes=[CP0[u]])
                        sc.op("dve", lambda e, u=u, q=q, z0=z0, c=c: e.scalar_tensor_tensor(out=acc[:, u, :], in0=zb[:, q, z0:z0 + 512], scalar=cv[:, cw + 16 + c:cw + 17 + c], in1=cp1[:, u, :], op0=ALU.mult, op1=ALU.add),
                              reads=zr + [CP1[u], R_cv], writes=[ACC[u]])
                        sc.op("dve", lambda e, u=u: e.tensor_tensor(out=acc[:, u, :], in0=acc[:, u, :], in1=cp0[:, u, :], op=ALU.add),
                              reads=[ACC[u], CP0[u]], writes=[ACC[u]])
                        sc.op("dve", lambda e, u=u, b0=b0, c=c, tl=tl: e.tensor_tensor(out=mT[:, c, tl * 512:(tl + 1) * 512], in0=PS[b0][:], in1=acc[:, u, :], op=ALU.mult),
                              reads=[PSR[b0], ACC[u]], writes=[MT[c][tl]])
                        dve_tick()
                    if hf == 0:
                        sc.op("pool", lambda e, q=q, c=c: e.tensor_copy(out=zc[:, c, :], in_=zb[:, q, 1024:1026]), reads=[ZB[q][2]], writes=[ZC[c]])
                    for slot in (sb, sg_, sv):
                        w_free(slot)
                it = 0
                pst = PostStats(scr, tts, banks=(4, 5))
                for dc in range(8):
                    so = w_next("wout")
                    for tl, tt in enumerate(tts):
                        bank = 6 + (it % 2)
                        it += 1
                        pst.tick()
                        mm_group(bank, [(ring[:, so, k * 128:(k + 1) * 128], mT[:, k, tl * 512:(tl + 1) * 512], [RING[so], MT[k][tl]]) for k in range(8)])
                        sc.op("dve", lambda e, dc=dc, tt=tt, bank=bank: e.tensor_tensor(out=hT[:, dc, tt * 512:(tt + 1) * 512], in0=hT[:, dc, tt * 512:(tt + 1) * 512], in1=PS[bank][:], op=ALU.add),
                              reads=[H[dc][tt], PSR[bank]], writes=[H[dc][tt]])
                        pst.push(dc, tl)
                    w_free(so)
                pst.finish()
                phase_end()

        def phase_ffn(gname, hf, nxt=None):
            tts = [2 * hf, 2 * hf + 1]
            with ExitStack() as st2:
                e2 = st2.enter_context
                aT = e2(sbt("aT", [128, FG, 1024], BF16))
                sgb = e2(sbt("sgb", [128, 2, 512], F32))
                AT = [[Region(f"aT{c}_{t}") for t in range(2)] for c in range(FG)]
                SG = [Region("sg0"), Region("sg1")]
                scr = norm_scratch(st2)
                xn, XN = get_xn(gname, tts, scr)
                schedule_prenorm(nxt)
                ita = 0
                itb = 0
                for g0 in range(0, NFC, FG):
                    fcs = list(range(g0, min(g0 + FG, NFC)))
                    for i, fc in enumerate(fcs):
                        sg_, su = w_next("w13"), w_next("w13")
                        for tl in range(2):
                            b0 = 2 * (ita % 2)
                            u = ita % 2
                            ita += 1
                            mm_group(b0, [(ring[:, sg_, k * 128:(k + 1) * 128], xn[:, k, tl * 512:(tl + 1) * 512], [RING[sg_], XN[k][tl]]) for k in range(8)])
                            mm_group(b0 + 1, [(ring[:, su, k * 128:(k + 1) * 128], xn[:, k, tl * 512:(tl + 1) * 512], [RING[su], XN[k][tl]]) for k in range(8)])
                            sc.op("act", lambda e, u=u, b0=b0: e.activation(out=sgb[:, u, :], in_=PS[b0][:], func=AF.Silu),
                                  reads=[PSR[b0]], writes=[SG[u]])
                            sc.op("dve", lambda e, u=u, b0=b0, i=i, tl=tl: e.tensor_tensor(out=aT[:, i, tl * 512:(tl + 1) * 512], in0=PS[b0 + 1][:], in1=sgb[:, u, :], op=ALU.mult),
                                  reads=[PSR[b0 + 1], SG[u]], writes=[AT[i][tl]])
                            dve_tick()
                        w_free(sg_)
                        w_free(su)
                    s2 = [w_next("w2") for _ in fcs]
                    lastg = (g0 + FG >= NFC)
                    pst = PostStats(scr, tts, banks=(2, 3), delay=3) if lastg else None
                    for tl, tt in enumerate(tts):
                        for dc in range(8):
                            bank = 4 + (itb % 4)
                            itb += 1
                            if lastg:
                                pst.tick()
                            mm_group(bank, [(ring[:, s2[i], dc * 128:(dc + 1) * 128], aT[:, i, tl * 512:(tl + 1) * 512], [RING[s2[i]], AT[i][tl]]) for i in range(len(fcs))])
                            sc.op("dve", lambda e, dc=dc, tt=tt, bank=bank: e.tensor_tensor(out=hT[:, dc, tt * 512:(tt + 1) * 512], in0=hT[:, dc, tt * 512:(tt + 1) * 512], in1=PS[bank][:], op=ALU.add),
                                  reads=[H[dc][tt], PSR[bank]], writes=[H[dc][tt]])
                            if lastg:
                                pst.push(dc, tl)
                    for s_ in s2:
                        w_free(s_)
                    if lastg:
                        pst.finish()
                phase_end()

        def phase_kv(hf, nxt=None):
            tts = [2 * hf, 2 * hf + 1]
            with ExitStack() as st2:
                e2 = st2.enter_context
                t12 = e2(sbt("t12", [64, 2, 512], F32))
                T12 = [Region("t1"), Region("t2")]
                scr = norm_scratch(st2)
                sq, SQ, lnv, LNV, rstd, RSTD = scr
                xn, XN = get_xn("kv_norm", tts, scr)
                schedule_prenorm(nxt)
                sl0, sl1, sr, srs = w_next("wdkv"), w_next("wdkv"), w_next("wdkvr"), w_next("wdkvr")
                for tl, tt in enumerate(tts):
                    xs = lambda k: xn[:, k, tl * 512:(tl + 1) * 512]
                    mm_group(0, [(ring[:, sl0, k * 128:(k + 1) * 128], xs(k), [RING[sl0], XN[k][tl]]) for k in range(8)])
                    mm_group(1, [(ring[:, sl1, k * 128:(k + 1) * 128], xs(k), [RING[sl1], XN[k][tl]]) for k in range(8)])
                    mm_group(2, [(ring[:, sr, k * 64:(k + 1) * 64], xs(k), [RING[sr], XN[k][tl]]) for k in range(8)], M=64)
                    mm_group(3, [(ring[:, srs, k * 64:(k + 1) * 64], xs(k), [RING[srs], XN[k][tl]]) for k in range(8)], M=64)
                    for kc in range(2):
                        sc.op("act", lambda e, kc=kc: e.activation(out=sq[:, kc, :], in_=PS[kc][:], func=AF.Square), reads=[PSR[kc]], writes=[SQ[kc]])
                    mm_group(4, [(ones[:], sq[:, kc, :], [R_ones, SQ[kc]]) for kc in range(2)])
                    ln_exp_rstd(PS[4][:], PSR[4], rstd[:, 0, :], RSTD[0], lnv[:, 0, :], LNV[0], 1.0 / KVL)
                    for kc in range(2):
                        sc.op("dve", lambda e, kc=kc: e.scalar_tensor_tensor(out=clT[:, kc, tt * 512:(tt + 1) * 512], in0=PS[kc][:], scalar=cvc("lat_norm", kc), in1=rstd[:, 0, :], op0=ALU.mult, op1=ALU.mult),
                              reads=[PSR[kc], RSTD[0], R_cv], writes=[CL[kc][tt]])
                    sc.op("dve", lambda e: e.tensor_tensor(out=t12[:, 0, :], in0=PS[2][0:64, :], in1=tabC[:, tt * 512:(tt + 1) * 512], op=ALU.mult),
                          reads=[PSR[2], R_tabC[tt]], writes=[T12[0]])
                    sc.op("dve", lambda e: e.tensor_tensor(out=t12[:, 1, :], in0=PS[3][0:64, :], in1=tabS[:, tt * 512:(tt + 1) * 512], op=ALU.mult),
                          reads=[PSR[3], R_tabS[tt]], writes=[T12[1]])
                    sc.op("dve", lambda e: e.tensor_tensor(out=kpe[0:64, tt * 512:(tt + 1) * 512], in0=t12[:, 0, :], in1=t12[:, 1, :], op=ALU.add),
                          reads=T12, writes=[KPE[tt]])
                for s_ in (sl0, sl1, sr, srs):
                    w_free(s_)
                phase_end()

        def phase_mla(j, hf, nxt=None):
            tts = [2 * hf, 2 * hf + 1]
            nk4 = 2 * (hf + 1)
            with ExitStack() as st2:
                e2 = st2.enter_context
                cqn = e2(sbt("cqn", [128, 3, 1024], BF16))
                CQN = [[Region(f"cqn{m}_{t}") for t in range(2)] for m in range(3)]
                KT = e2(sbt("KT", [128, 2, S], BF16))
                KTR = [[Region(f"KT{q}_{g}") for g in range(4)] for q in range(2)]
                V = e2(sbt("V", [128, 2, 16, 128], BF16))
                VR = [[Region(f"V{q}_{g}") for g in range(4)] for q in range(2)]
                qn = e2(sbt("qn", [128, 2, 512], BF16))
                QN = [Region("qn0"), Region("qn1")]
                qpe = e2(sbt("qpe", [128, 2, 512], BF16))
                QPE = [Region("qpe0"), Region("qpe1")]
                t12 = e2(sbt("t12", [64, 2, 512], F32))
                T12 = [Region("t1"), Region("t2")]
                Pb = e2(sbt("Pb", [128, 4, 512], BF16))
                PR = [Region(f"P{i}") for i in range(4)]
                rec = e2(sbt("rec", [128, 512], F32))
                REC = Region("rec")
                pdsb = e2(sbt("pdsb", [128, 512], F32))
                PDSB = Region("pdsb")
                scr = norm_scratch(st2)
                sq, SQ, lnv, LNV, rstd, RSTD = scr
                sc.op("pool", lambda e: e.memset(qpe[:], 0.0), writes=QPE)
                xn, XN = get_xn(f"mla_norm{j}", tts, scr)
                attnT, ATT = xn, XN
                schedule_prenorm(nxt)
                sdq = [w_next("wdq") for _ in range(3)]
                for tl, tt in enumerate(tts):
                    for m in range(3):
                        mm_group(m, [(ring[:, sdq[m], k * 128:(k + 1) * 128], xn[:, k, tl * 512:(tl + 1) * 512], [RING[sdq[m]], XN[k][tl]]) for k in range(8)])
                    for m in range(3):
                        q = m % 2
                        sc.op("act", lambda e, m=m, q=q: e.activation(out=sq[:, q, :], in_=PS[m][:], func=AF.Square), reads=[PSR[m]], writes=[SQ[q]])
                        sc.op("pe", lambda e, m=m, q=q: e.matmul(PS[3][:], ones[:], sq[:, q, :], start=(m == 0), stop=(m == 2)),
                              reads=[SQ[q], R_ones], writes=[PSR[3]], signal=True)
                    ln_exp_rstd(PS[3][:], PSR[3], rstd[:, 0, :], RSTD[0], lnv[:, 0, :], LNV[0], 1.0 / QL)
                    for m in range(3):
                        sc.op("dve", lambda e, m=m: e.scalar_tensor_tensor(out=cqn[:, m, tl * 512:(tl + 1) * 512], in0=PS[m][:], scalar=cvc(f"q_norm{j}", m), in1=rstd[:, 0, :], op0=ALU.mult, op1=ALU.mult),
                              reads=[PSR[m], RSTD[0], R_cv], writes=[CQN[m][tl]])
                for s_ in sdq:
                    w_free(s_)

                suq = {}
                bg = []
                mctr = [0]

                def misc_bank():
                    mctr[0] += 1
                    return 6 + (mctr[0] % 2)

                def kvgen_items(h):
                    q = h % 2
                    s_ = w_next("wukv")
                    items = []

                    def kt_item(g):
                        bank = misc_bank()
                        mm_group(bank, [(ring[:, s_, kc * 256:kc * 256 + 128], clT[:, kc, g * 512:(g + 1) * 512], [RING[s_], CL[kc][g]]) for kc in range(2)])
                        sc.op("dve", lambda e: e.tensor_copy(out=KT[:, q, g * 512:(g + 1) * 512], in_=PS[bank][:]),
                              reads=[PSR[bank]], writes=[KTR[q][g]])

                    def v_item(g, last):
                        bank = misc_bank()
                        for i4 in range(4):
                            kt = g * 4 + i4
                            for kc in range(2):
                                lastmm = (i4 == 3 and kc == 1)
                                sc.op("pe", lambda e, kc=kc, kt=kt, i4=i4: e.matmul(
                                    PS[bank][:, i4 * 128:(i4 + 1) * 128], clT[:, kc, kt * 128:(kt + 1) * 128],
                                    ring[:, s_, kc * 256 + 128:kc * 256 + 256], start=(kc == 0), stop=(kc == 1)),
                                    reads=[RING[s_], CL[kc][g]], writes=[PSR[bank]], signal=lastmm)
                        sc.op("dve", lambda e: e.tensor_copy(out=V[:, q, g * 4:(g + 1) * 4, :], in_=PS[bank][:].rearrange("p (a b) -> p a b", a=4)),
                              reads=[PSR[bank]], writes=[VR[q][g]])
                        if last:
                            w_free(s_)

                    for g in range(nk4):
                        items.append(lambda g=g: kt_item(g))
                    for g in range(nk4):
                        items.append(lambda g=g: v_item(g, g == nk4 - 1))
                    return items

                def qproj_items(n):
                    h, tl = n // 2, n % 2
                    tt = tts[tl]
                    u = n % 2
                    if tl == 0:
                        suq[h] = w_next("wuq")
                    s_ = suq[h]
                    cq = lambda m: cqn[:, m, tl * 512:(tl + 1) * 512]

                    def a():
                        bank = misc_bank()
                        mm_group(bank, [(ring[:, s_, m * 256:m * 256 + 128], cq(m), [RING[s_], CQN[m][tl]]) for m in range(3)])
                        sc.op("dve", lambda e: e.tensor_copy(out=qn[:, u, :], in_=PS[bank][:]), reads=[PSR[bank]], writes=[QN[u]])
                        if tl == 1:
                            w_free(s_)

                    def b():
                        bank = misc_bank()
                        mm_group(bank, [(ring[:, s_, m * 256 + 128:m * 256 + 192], cq(m), [RING[s_], CQN[m][tl]]) for m in range(3)], M=64)
                        sc.op("dve", lambda e: e.tensor_tensor(out=t12[:, 0, :], in0=PS[bank][0:64, :], in1=tabC[:, tt * 512:(tt + 1) * 512], op=ALU.mult),
                              reads=[PSR[bank], R_tabC[tt]], writes=[T12[0]])

                    def c():
                        bank = misc_bank()
                        mm_group(bank, [(ring[:, s_, m * 256 + 192:m * 256 + 256], cq(m), [RING[s_], CQN[m][tl]]) for m in range(3)], M=64)
                        sc.op("dve", lambda e: e.tensor_tensor(out=t12[:, 1, :], in0=PS[bank][0:64, :], in1=tabS[:, tt * 512:(tt + 1) * 512], op=ALU.mult),
                              reads=[PSR[bank], R_tabS[tt]], writes=[T12[1]])
                        sc.op("dve", lambda e: e.tensor_tensor(out=qpe[0:64, u, :], in0=t12[:, 0, :], in1=t12[:, 1, :], op=ALU.add),
                              reads=T12, writes=[QPE[u]])

                    return [b, c, a]

                def bg_step():
                    if bg:
                        bg.pop(0)()

                def attention(n):
                    h, tl = n // 2, n % 2
                    tt = tts[tl]
                    u = n % 2
                    q = h % 2
                    bo, bd = 3 + (n % 2), 5
                    nkt = 4 * (tt + 1)

                    def s_mm(i):
                        d = i - 4 * tt
                        q0 = 128 * d if d > 0 else 0
                        sb = i % 3
                        g = i // 4
                        sc.op("pe", lambda e: e.matmul(PS[sb][:, q0:512], KT[:, q, i * 128:(i + 1) * 128], qn[:, u, q0:512], start=True, stop=False),
                              reads=[KTR[q][g], QN[u]], writes=[PSR[sb]], signal=False)
                        sc.op("pe", lambda e: e.matmul(PS[sb][:, q0:512], kpe[:, i * 128:(i + 1) * 128], qpe[:, u, q0:512], start=False, stop=(d < 0)),
                              reads=[KPE[g], QPE[u]], writes=[PSR[sb]], signal=(d < 0))
                        if d >= 0:
                            sc.op("pe", lambda e: e.matmul(PS[sb][:, q0:q0 + 128], ident[:], mneg[:], start=False, stop=True),
                                  reads=[R_id], writes=[PSR[sb]], signal=True)

                    def exp_p(i):
                        d = i - 4 * tt
                        q0 = 128 * d if d > 0 else 0
                        sb = i % 3
                        pi = i % 4
                        sc.op("act", lambda e: e.activation(out=Pb[:, pi, q0:512], in_=PS[sb][:, q0:512], func=AF.Exp, scale=SCALE),
                              reads=[PSR[sb]], writes=[PR[pi]])

                    def pv_mm(i):
                        d = i - 4 * tt
                        q0 = 128 * d if d > 0 else 0
                        sb = i % 3
                        pi = i % 4
                        g = i // 4
                        sc.op("pe", lambda e: e.matmul(PS[bo][:, q0:512], V[:, q, i, :], Pb[:, pi, q0:512], start=(i == 0), stop=(i == nkt - 1)),
                              reads=[VR[q][g], PR[pi]], writes=[PSR[bo]], signal=False)
                        sc.op("pe", lambda e: e.matmul(PS[bd][:, q0:512], ones[:], Pb[:, pi, q0:512], start=(i == 0), stop=(i == nkt - 1)),
                              reads=[R_ones, PR[pi]], writes=[PSR[bd]], signal=True)

                    def prologue():
                        s_mm(0)
                        if nkt > 1:
                            s_mm(1)
                        exp_p(0)

                    def body():
                        for i in range(nkt):
                            if i + 2 < nkt:
                                s_mm(i + 2)
                            if i + 1 < nkt:
                                exp_p(i + 1)
                            pv_mm(i)
                            bg_step()
                            if nkt <= 4:
                                bg_step()
                            dve_tick()

                    def finalize():
                        sc.op("act", lambda e: e.activation(out=pdsb[:], in_=PS[bd][:], func=AF.Ln), reads=[PSR[bd]], writes=[PDSB])
                        sc.op("act", lambda e: e.activation(out=rec[:], in_=pdsb[:], func=AF.Exp, scale=-1.0), reads=[PDSB], writes=[REC])
                        sc.op("dve", lambda e: e.tensor_tensor(out=attnT[:, h, tl * 512:(tl + 1) * 512], in0=PS[bo][:], in1=rec[:], op=ALU.mult),
                              reads=[PSR[bo], REC], writes=[ATT[h][tl]])

                    return prologue, body, finalize

                for it_ in kvgen_items(0):
                    it_()
                for it_ in qproj_items(0):
                    it_()
                parts = attention(0)
                parts[0]()
                for n in range(16):
                    h, tl = n // 2, n % 2
                    if tl == 0 and h + 1 < NH:
                        bg.extend(kvgen_items(h + 1))
                    if n + 1 < 16:
                        qitems = qproj_items(n + 1)
                    else:
                        qitems = []
                    if tl == 1:
                        bg.extend(qitems)
                        parts[1]()
                        while bg:
                            bg_step()
                    else:
                        bg[0:0] = qitems
                        parts[1]()
                        while any(it_ in bg for it_ in qitems):
                            bg_step()
                    nparts = attention(n + 1) if n + 1 < 16 else None
                    if nparts is not None:
                        nparts[0]()
                    parts[2]()
                    parts = nparts
                it = 0
                pst = PostStats(scr, tts, banks=(4, 5))
                for dc in range(8):
                    so = w_next("wo")
                    for tl, tt in enumerate(tts):
                        bank = 6 + (it % 2)
                        it += 1
                        pst.tick()
                        mm_group(bank, [(ring[:, so, k * 128:(k + 1) * 128], attnT[:, k, tl * 512:(tl + 1) * 512], [RING[so], ATT[k][tl]]) for k in range(8)])
                        sc.op("dve", lambda e, dc=dc, tt=tt, bank=bank: e.tensor_tensor(out=hT[:, dc, tt * 512:(tt + 1) * 512], in0=hT[:, dc, tt * 512:(tt + 1) * 512], in1=PS[bank][:], op=ALU.add),
                              reads=[H[dc][tt], PSR[bank]], writes=[H[dc][tt]])
                        pst.push(dc, tl)
                    w_free(so)
                pst.finish()
                phase_end()

        def phase_final(s, hf):
            tts = [2 * hf, 2 * hf + 1]
            with ExitStack() as st2:
                scr = norm_scratch(st2)
                yst = st2.enter_context(sbt("yst", [128, 8, 1024], F32))
                YST = [Region(f"yst{c}") for c in range(8)]
                for tl, tt in enumerate(tts):
                    if not rstd_valid[tt]:
                        stats_from_h(tt, scr, 6 + (tl % 2))
                for c in range(8):
                    for tl, tt in enumerate(tts):
                        sc.op("dve", lambda e, c=c, tl=tl, tt=tt: e.scalar_tensor_tensor(
                            out=yst[:, c, tl * 512:(tl + 1) * 512], in0=hT[:, c, tt * 512:(tt + 1) * 512],
                            scalar=cvc("final_norm", c), in1=rstdP[:, tt, :], op0=ALU.mult, op1=ALU.mult),
                            reads=[H[c][tt], RSTDP[tt], R_cv], writes=[YST[c]])
                    sc.dma("sp", "st", yT[s, c * 128:(c + 1) * 128, hf * 1024:(hf + 1) * 1024], yst[:, c, :], reads=[YST[c]])
                if s + 1 < n_seq:
                    phase_load(s + 1, hf)
                d_ = sc.dsem["st"]
                for e_ in ("sp", "dve", "act", "pool"):
                    sc._wait(e_, "st", d_[1])
                sc.barrier()

        phases = []
        for l in range(2):
            for hf in range(2):
                phases.append(("conv", l, hf))
            for hf in range(2):
                phases.append(("ffn", f"cffn_norm{l}", hf))
        for hf in range(2):
            phases.append(("kv", hf))
        for j in range(2):
            for hf in range(2):
                phases.append(("mla", j, hf))
            for hf in range(2):
                phases.append(("ffn", f"mffn_norm{j}", hf))
        if n_phases is not None:
            phases = phases[:n_phases]
        full = (n_phases is None)
        if not full and n_seq > 1:
            raise NotImplementedError("truncated program supports n_seq=1 only")
        wstate["cap"] = NT * n_seq if full else _tiles_for_phases(phases)
        w_pump()

        def norm_info(ph):
            hf = ph[-1]
            tts = [2 * hf, 2 * hf + 1]
            if ph[0] == "conv":
                return (f"conv_norm{ph[1]}", tts)
            if ph[0] == "ffn":
                return (ph[1], tts)
            if ph[0] == "kv":
                return ("kv_norm", tts)
            return (f"mla_norm{ph[1]}", tts)

        def run_phase(ph, nxt):
            nxt = norm_info(nxt) if nxt is not None else None
            if ph[0] == "conv":
                phase_conv(ph[1], ph[2], nxt)
            elif ph[0] == "ffn":
                phase_ffn(ph[1], ph[2], nxt)
            elif ph[0] == "kv":
                phase_kv(ph[1], nxt)
            elif ph[0] == "mla":
                phase_mla(ph[1], ph[2], nxt)

        phase_load(0, 0)
        phase_load(0, 1)
        pool_bg.extend(table_steps(0))
        last_mla = max([i for i, ph in enumerate(phases) if ph[0] == "mla"], default=None)
        for s in range(n_seq):
            for ip, ph in enumerate(phases):
                if ph[0] == "kv":
                    flush_pool_bg()
                run_phase(ph, phases[ip + 1] if ip + 1 < len(phases) else None)
                if s + 1 < n_seq and ip == last_mla:
                    pool_bg.extend(table_steps(s + 1))
            flush_pool_bg()
            for hf in range(2):
                phase_final(s, hf)
        d = sc.dsem["st"]
        nc.sync.wait_ge(d[0], d[1])
        build_program.last_nins = sc.nins
    return nc, plan


def _tiles_for_phases(phases):
    n = 0
    for ph in phases:
        if ph[0] == "conv":
            n += 24 + 8
        elif ph[0] == "ffn":
            n += 66
        elif ph[0] == "kv":
            n += 4
        elif ph[0] == "mla":
            n += 3 + 16 + 8
    return n


def _weight_plan():
    plan = []

    def ffn(w13, w2, l):
        for hf in range(2):
            for g0 in range(0, NFC, FG):
                fcs = list(range(g0, min(g0 + FG, NFC)))
                for fc in fcs:
                    plan.append(dict(kind="w13", w=w13, l=l, K=D, cols=np.arange(fc * 128, fc * 128 + 128), L=1024))
                    plan.append(dict(kind="w13", w=w13, l=l, K=D, cols=np.arange(DFF + fc * 128, DFF + fc * 128 + 128), L=1024))
                for fc in fcs:
                    plan.append(dict(kind="w2", w=w2, l=l, rows=(fc * 128, fc * 128 + 128), L=1024))

    for l in range(2):
        for hf in range(2):
            for c in range(8):
                for off in (0, D, 2 * D):
                    plan.append(dict(kind="win", w="conv_w_in", l=l, K=D, cols=np.arange(off + c * 128, off + c * 128 + 128), L=1024))
            for dc in range(8):
                plan.append(dict(kind="wout", w="conv_w_out", l=l, K=D, cols=np.arange(dc * 128, dc * 128 + 128), L=1024))
        ffn("conv_ffn_w13", "conv_ffn_w2", l)
    for hf in range(2):
        plan.append(dict(kind="wdkv", w="w_dkv", l=None, K=D, cols=np.arange(0, 128), L=1024))
        plan.append(dict(kind="wdkv", w="w_dkv", l=None, K=D, cols=np.arange(128, 256), L=1024))
        plan.append(dict(kind="wdkvr", w="w_dkv", l=None, K=D, cols=np.arange(256, 320), L=512))
        plan.append(dict(kind="wdkvr", w="w_dkv", l=None, K=D, cols=np.concatenate([np.arange(288, 320), np.arange(256, 288)]), L=512))
    for j in range(2):
        for hf in range(2):
            for m in range(3):
                plan.append(dict(kind="wdq", w="mla_w_dq", l=j, K=D, cols=np.arange(m * 128, m * 128 + 128), L=1024))
            order = [("wukv", 0), ("wuq", 0)]
            for n in range(16):
                h, tl = n // 2, n % 2
                if tl == 0 and h + 1 < NH:
                    order.append(("wukv", h + 1))
                if n + 1 < 16 and (n + 1) % 2 == 0:
                    order.append(("wuq", (n + 1) // 2))
            for kind, h in order:
                if kind == "wukv":
                    plan.append(dict(kind="wukv", w="w_ukv", l=None, K=KVL, cols=np.arange(h * 256, h * 256 + 256), L=512))
                else:
                    b = h * 192
                    cols = np.concatenate([np.arange(b, b + 128), np.arange(b + 128, b + 192), np.arange(b + 160, b + 192), np.arange(b + 128, b + 160)])
                    plan.append(dict(kind="wuq", w="mla_w_uq", l=j, K=QL, cols=cols, L=768))
            for dc in range(8):
                plan.append(dict(kind="wo", w="mla_w_o", l=j, K=D, cols=np.arange(dc * 128, dc * 128 + 128), L=1024))
        ffn("mla_ffn_w13", "mla_ffn_w2", j)
    return plan


def _dedupe(plan):
    seen = {}
    for sp in plan:
        key = (sp["w"], sp["l"], tuple(sp["rows"]) if "rows" in sp else tuple(int(c) for c in sp["cols"]))
        if key not in seen:
            seen[key] = len(seen)
        sp["u"] = seen[key]
    return len(seen)


def _make_wst(plan, inputs):
    NU = _dedupe(plan)
    wst = np.zeros((NU, 128, 1024), np.float32)
    done = set()
    for sp in plan:
        i = sp["u"]
        if i in done:
            continue
        done.add(i)
        W = inputs[sp["w"]]
        if sp["l"] is not None:
            W = W[sp["l"]]
        if "rows" in sp:
            r0, r1 = sp["rows"]
            wst[i, :, :] = W[r0:r1, :]
        else:
            K = sp["K"]
            cols = sp["cols"]
            kc = K // 128
            t = W[:, cols].reshape(kc, 128, len(cols)).transpose(1, 0, 2).reshape(128, kc * len(cols))
            wst[i, :, :t.shape[1]] = t
    return wst


def _make_cvec(inputs):
    cv = np.zeros((128, NCV), np.float32)

    def put(name, vec):
        n = vec.shape[0] // 128
        cv[:, CV[name]:CV[name] + n] = np.asarray(vec, np.float32).reshape(n, 128).T

    for l in range(2):
        put(f"conv_norm{l}", inputs["conv_norm_g"][l])
        put(f"cffn_norm{l}", inputs["conv_ffn_norm_g"][l])
        put(f"mla_norm{l}", inputs["mla_norm_g"][l])
        put(f"mffn_norm{l}", inputs["mla_ffn_norm_g"][l])
        put(f"q_norm{l}", inputs["mla_q_norm_g"][l])
        for k in range(3):
            cv[:, CV[f"convw{l}"] + 8 * k:CV[f"convw{l}"] + 8 * k + 8] = np.asarray(inputs["conv_w"][l][k], np.float32).reshape(8, 128).T
    put("kv_norm", inputs["kv_norm_g"])
    put("final_norm", inputs["final_norm_g"])
    put("lat_norm", inputs["kv_latent_norm_g"])
    invf = (np.float32(10000.0) ** (-np.arange(0, ROPE, 2, dtype=np.float32) / np.float32(ROPE))).astype(np.float32)
    p = np.arange(128)
    cv[:, CV["invf"]] = invf[p % 32]
    cv[:, CV["phc"]] = np.float32(np.pi / 2)
    cv[:, CV["phs"]] = np.where((p % 64) < 32, np.float32(np.pi), np.float32(0.0))
    cv[:, CV["eps"]] = np.float32(EPS)
    return cv


_PROG_CACHE = {}


def _get_program(n_phases=None, n_seq=NSEQ):
    key = (n_phases, n_seq)
    if key not in _PROG_CACHE:
        _PROG_CACHE[key] = build_program(n_phases, n_seq)
    return _PROG_CACHE[key]


def kernel(**inputs):
    n_phases = inputs.pop("_n_phases", None)
    n_seq = inputs.pop("_n_seq", NSEQ)
    inputs = {k: np.asarray(v) for k, v in inputs.items()}
    nc, plan = _get_program(n_phases, n_seq)
    wst = _make_wst(plan, inputs)
    cvec = _make_cvec(inputs)
    x = inputs["x"]
    posn = inputs["positions"].astype(np.int32)
    n_cores = 8
    in_maps = []
    for c in range(n_cores):
        xs = np.ascontiguousarray(x[NSEQ * c:NSEQ * (c + 1)].transpose(0, 2, 1))
        in_maps.append({"xT": xs, "pos": np.ascontiguousarray(posn[NSEQ * c:NSEQ * (c + 1)]), "cvec": cvec, "wst": wst})
    res = run_bass_kernel_spmd(nc, in_maps, core_ids=list(range(n_cores)))
    out = np.empty_like(x)
    for c in range(n_cores):
        out[NSEQ * c:NSEQ * (c + 1)] = res.results[c]["yT"].transpose(0, 2, 1)
    return out
```

```python
import math
from contextlib import ExitStack

import numpy as np
import concourse.bass as bass
import concourse.mybir as mybir
from concourse.bass_utils import run_bass_kernel_spmd

F32 = mybir.dt.float32
BF16 = mybir.dt.bfloat16
I32 = mybir.dt.int32
AF = mybir.ActivationFunctionType
ALU = mybir.AluOpType

D = 1024
S = 2048
NSEQ = 2
DFF = 2816
NFC = DFF // 128
NH = 8
KVL = 256
QL = 384
ROPE = 64
EPS = 1e-6
SCALE = (128 + 64) ** -0.5
R_SLOTS = 8
FG = 4
TWO_PI = float(2 * np.pi)

CV = {}
_n = 0
for _name, _w in [("conv_norm0", 8), ("conv_norm1", 8), ("cffn_norm0", 8), ("cffn_norm1", 8),
                  ("kv_norm", 8), ("mla_norm0", 8), ("mla_norm1", 8), ("mffn_norm0", 8),
                  ("mffn_norm1", 8), ("final_norm", 8), ("lat_norm", 2), ("q_norm0", 3),
                  ("q_norm1", 3), ("convw0", 24), ("convw1", 24), ("invf", 1), ("phc", 1),
                  ("phs", 1), ("eps", 1)]:
    CV[_name] = _n
    _n += _w
NCV = _n


class Region:
    __slots__ = ("name", "w", "r")

    def __init__(self, name):
        self.name = name
        self.w = None
        self.r = {}


class Sched:
    def __init__(self, nc, stack):
        self.nc = nc
        self.E = {"pe": nc.tensor, "act": nc.scalar, "dve": nc.vector, "pool": nc.gpsimd, "sp": nc.sync}
        self.csem = {e: stack.enter_context(nc.semaphore("c_" + e)) for e in ("pe", "act", "dve", "pool")}
        self.cnt = {e: 0 for e in self.csem}
        self.seen = {e: {} for e in self.E}
        self.dsem = {}
        self.stack = stack
        self.pe_unsignaled = False
        self.nins = 0

    def dma_sem(self, name):
        if name not in self.dsem:
            self.dsem[name] = [self.stack.enter_context(self.nc.semaphore("d_" + name)), 0]
        return self.dsem[name]

    def _wait(self, eng, key, val):
        if key in self.csem:
            if key == "pe" and val > self.cnt["pe"]:
                raise RuntimeError("wait on unsignaled PE event")
            sem = self.csem[key]
        else:
            sem, val = self.dsem[key]
        if self.seen[eng].get(key, 0) >= val:
            return
        self.E[eng].wait_ge(sem, val)
        self.nins += 1
        self.seen[eng][key] = val

    def _hazards(self, eng, reads, writes, is_dma):
        need = {}

        def add(key, val, raw):
            if key == eng and not is_dma:
                if eng == "pe" or not raw:
                    return
            if need.get(key, 0) < val:
                need[key] = val

        for r in reads:
            if r.w is not None:
                add(r.w[0], r.w[1], True)
        for w in writes:
            if w.w is not None:
                add(w.w[0], w.w[1], False)
            for k, v in w.r.items():
                add(k, v, False)
        for k, v in need.items():
            self._wait(eng, k, v)

    def op(self, eng, fn, reads=(), writes=(), signal=True):
        self._hazards(eng, reads, writes, False)
        ins = fn(self.E[eng])
        self.nins += 1
        if signal:
            ins.then_inc(self.csem[eng], 1)
            self.cnt[eng] += 1
            val = self.cnt[eng]
            if eng == "pe":
                self.pe_unsignaled = False
        else:
            assert eng == "pe"
            val = self.cnt[eng] + 1
            self.pe_unsignaled = True
        for r in reads:
            if r.r.get(eng, 0) < val:
                r.r[eng] = val
        for w in writes:
            w.w = (eng, val)
            w.r = {}

    def dma(self, qeng, semname, out_ap, in_ap, reads=(), writes=()):
        self._hazards(qeng, reads, writes, True)
        d = self.dma_sem(semname)
        self.E[qeng].dma_start(out=out_ap, in_=in_ap).then_inc(d[0], 16)
        self.nins += 1
        d[1] += 16
        for r in reads:
            r.r[semname] = d[1]
        for w in writes:
            w.w = (semname, d[1])
            w.r = {}

    def barrier(self):
        assert not self.pe_unsignaled
        for e in ("act", "dve", "pool", "sp"):
            for k in self.csem:
                if k != e:
                    self._wait(e, k, self.cnt[k])


def build_program(n_phases=None, n_seq=NSEQ):
    nc = bass.Bass("TRN2", target_bir_lowering=False)
    xT = nc.dram_tensor("xT", [NSEQ, D, S], F32, kind="ExternalInput").ap()
    pos = nc.dram_tensor("pos", [NSEQ, S], I32, kind="ExternalInput").ap()
    cvec_d = nc.dram_tensor("cvec", [128, NCV], F32, kind="ExternalInput").ap()
    wspecs = []
    plan = _weight_plan()
    NT = len(plan)
    NU = _dedupe(plan)
    wst = nc.dram_tensor("wst", [NU, 128, 1024], F32, kind="ExternalInput").ap()
    yT = nc.dram_tensor("yT", [NSEQ, D, S], F32, kind="ExternalOutput").ap()

    _uid = [0]

    def sbt(name, shape, dtype):
        _uid[0] += 1
        return nc.sbuf_tensor(f"{name}_{_uid[0]}", shape, dtype)

    with ExitStack() as stack:
        sc = Sched(nc, stack)
        ec = stack.enter_context
        hT = ec(nc.sbuf_tensor("hT", [128, 8, S], F32))
        ring = ec(nc.sbuf_tensor("ring", [128, R_SLOTS, 1024], BF16))
        tabC = ec(nc.sbuf_tensor("tabC", [64, S], F32))
        tabS = ec(nc.sbuf_tensor("tabS", [64, S], F32))
        clT = ec(nc.sbuf_tensor("clT", [128, 2, S], BF16))
        kpe = ec(nc.sbuf_tensor("kpe", [128, S], BF16))
        ident = ec(nc.sbuf_tensor("ident", [128, 128], BF16))
        mneg = ec(nc.sbuf_tensor("mneg", [128, 128], BF16))
        cv = ec(nc.sbuf_tensor("cv", [128, NCV], F32))
        ones = ec(nc.sbuf_tensor("ones", [128, 128], BF16))
        zc = ec(nc.sbuf_tensor("zc", [128, 8, 2], F32))
        TW = 256
        tposi = ec(nc.sbuf_tensor("tposi", [64, TW], I32))
        tang = ec(nc.sbuf_tensor("tang", [64, TW], F32))
        ta2 = ec(nc.sbuf_tensor("ta2", [64, TW], F32))
        ta2b = ec(nc.sbuf_tensor("ta2b", [64, TW], F32))
        tki = ec(nc.sbuf_tensor("tki", [64, TW], I32))
        tkf = ec(nc.sbuf_tensor("tkf", [64, TW], F32))
        rstdP = ec(nc.sbuf_tensor("rstdP", [128, 4, 512], F32))
        RSTDP = [Region(f"rstdP{t}") for t in range(4)]
        rstd_valid = [False] * 4
        T_R = {k: Region(k) for k in ("tposi", "tang", "ta2", "ta2b", "tki", "tkf")}
        PS = [ec(nc.psum_tensor(f"ps{i}", [128, 512], F32)) for i in range(8)]

        H = [[Region(f"H{c}_{t}") for t in range(4)] for c in range(8)]
        PSR = [Region(f"PS{i}") for i in range(8)]
        RING = [Region(f"ring{i}") for i in range(R_SLOTS)]
        R_tabC = [Region(f"tabC{t}") for t in range(4)]
        R_tabS = [Region(f"tabS{t}") for t in range(4)]
        CL = [[Region(f"cl{k}_{t}") for t in range(4)] for k in range(2)]
        KPE = [Region(f"kpe{t}") for t in range(4)]
        R_cv, R_ones = Region("cv"), Region("ones")
        ZC = [Region(f"zc{c}") for c in range(8)]

        wstate = {"issued": 0, "consumed": 0, "cap": 0}
        occupant = {}
        freed = set()

        def w_pump():
            while wstate["issued"] < wstate["cap"]:
                n = wstate["issued"]
                if n >= R_SLOTS and (n - R_SLOTS) not in freed:
                    break
                spec = plan[n % NT]
                L = spec["L"]
                slot = n % R_SLOTS
                sc.dma("pool", f"ring{slot}", ring[:, slot, 0:L], wst[spec["u"], :, 0:L], writes=[RING[slot]])
                wstate["issued"] += 1

        def w_next(kind):
            n = wstate["consumed"]
            spec = plan[n % NT]
            assert spec["kind"] == kind, (spec["kind"], kind, n)
            assert n < wstate["issued"], "weight tile consumed before its DMA was issued"
            wstate["consumed"] += 1
            occupant[n % R_SLOTS] = n
            return n % R_SLOTS

        def w_free(slot):
            freed.add(occupant[slot])
            w_pump()
            if pool_bg:
                pool_bg.pop(0)()

        sc.dma("sp", "cvld", cv[:], cvec_d[:, :], writes=[R_cv])
        sc.op("dve", lambda e: e.memset(ones[:], 1.0), writes=[R_ones])
        R_id = Region("ident")
        sc.op("pool", lambda e: e.memset(kpe[:], 0.0), writes=KPE)
        sc.op("pool", lambda e: e.memset(ident[:], 0.0), writes=[R_id])
        sc.op("pool", lambda e: e.affine_select(out=ident[:], in_=ident[:], pattern=[[-1, 128]], compare_op=ALU.not_equal, fill=1.0, base=0, channel_multiplier=1),
              reads=[R_id], writes=[R_id])
        sc.op("pool", lambda e: e.memset(mneg[:], 0.0), writes=[R_id])
        sc.op("pool", lambda e: e.affine_select(out=mneg[:], in_=mneg[:], pattern=[[1, 128]], compare_op=ALU.is_ge, fill=-30000.0, base=0, channel_multiplier=-1),
              reads=[R_id], writes=[R_id])

        def cvc(name, j=0):
            c = CV[name] + j
            return cv[:, c:c + 1]

        def ln_exp_rstd(ps_ap, ps_reg, rstd_ap, rstd_reg, lnv_ap, lnv_reg, inv_d, P=128):
            sc.op("act", lambda e: e.activation(out=lnv_ap, in_=ps_ap, func=AF.Ln, bias=cv[0:P, CV["eps"]:CV["eps"] + 1], scale=inv_d),
                  reads=[ps_reg, R_cv], writes=[lnv_reg])
            sc.op("act", lambda e: e.activation(out=rstd_ap, in_=lnv_ap, func=AF.Exp, scale=-0.5),
                  reads=[lnv_reg], writes=[rstd_reg])

        def stats_from_h(tt, scr, bank):
            sq, SQ, lnv, LNV, rstd, RSTD = scr
            for c in range(8):
                q = c % 4
                sc.op("act", lambda e, c=c, q=q: e.activation(out=sq[:, q, :], in_=hT[:, c, tt * 512:(tt + 1) * 512], func=AF.Square),
                      reads=[H[c][tt]], writes=[SQ[q]])
                sc.op("pe", lambda e, c=c, q=q: e.matmul(PS[bank][:], ones[:], sq[:, q, :], start=(c == 0), stop=(c == 7)),
                      reads=[SQ[q], R_ones], writes=[PSR[bank]], signal=True)
            ln_exp_rstd(PS[bank][:], PSR[bank], rstdP[:, tt, :], RSTDP[tt], lnv[:, 0, :], LNV[0], 1.0 / D)
            rstd_valid[tt] = True

        def norm_h(gname, tts, xn, XN, scr):
            for tl, tt in enumerate(tts):
                if not rstd_valid[tt]:
                    stats_from_h(tt, scr, 6 + (tl % 2))
                for c in range(8):
                    sc.op("dve", lambda e, c=c: e.scalar_tensor_tensor(
                        out=xn[:, c, tl * 512:(tl + 1) * 512], in0=hT[:, c, tt * 512:(tt + 1) * 512],
                        scalar=cvc(gname, c), in1=rstdP[:, tt, :], op0=ALU.mult, op1=ALU.mult),
                        reads=[H[c][tt], RSTDP[tt], R_cv], writes=[XN[c][tl]])

        class PostStats:
            def __init__(self, scr, tts, banks=(0, 1), delay=2):
                self.scr = scr
                self.tts = tts
                self.banks = banks
                self.delay = delay
                self.q = []
                self.n = 0
                self.cnt = {tl: 0 for tl in range(len(tts))}

            def push(self, dc, tl):
                sq, SQ = self.scr[0], self.scr[1]
                tt = self.tts[tl]
                slot = self.n % 4
                self.n += 1
                sc.op("act", lambda e: e.activation(out=sq[:, slot, :], in_=hT[:, dc, tt * 512:(tt + 1) * 512], func=AF.Square),
                      reads=[H[dc][tt]], writes=[SQ[slot]])
                bank = self.banks[tl]
                k = self.cnt[tl]
                self.cnt[tl] += 1

                def mm():
                    sc.op("pe", lambda e: e.matmul(PS[bank][:], ones[:], sq[:, slot, :], start=(k == 0), stop=(k == 7)),
                          reads=[SQ[slot], R_ones], writes=[PSR[bank]], signal=True)
                self.q.append(mm)

            def tick(self):
                while len(self.q) > self.delay:
                    self.q.pop(0)()

            def finish(self):
                lnv, LNV = self.scr[2], self.scr[3]
                while self.q:
                    self.q.pop(0)()
                for tl, tt in enumerate(self.tts):
                    assert self.cnt[tl] == 8
                    b_ = self.banks[tl]
                    sc.op("act", lambda e, b_=b_, tl=tl: e.activation(out=lnv[:, tl % 2, :], in_=PS[b_][:], func=AF.Ln, bias=cv[:, CV["eps"]:CV["eps"] + 1], scale=1.0 / D),
                          reads=[PSR[b_], R_cv], writes=[LNV[tl % 2]])
                for tl, tt in enumerate(self.tts):
                    sc.op("act", lambda e, tl=tl, tt=tt: e.activation(out=rstdP[:, tt, :], in_=lnv[:, tl % 2, :], func=AF.Exp, scale=-0.5),
                          reads=[LNV[tl % 2]], writes=[RSTDP[tt]])
                    rstd_valid[tt] = True

        def norm_scratch(stack2):
            sq = stack2.enter_context(sbt("sq", [128, 4, 512], BF16))
            lnv = stack2.enter_context(sbt("lnv", [128, 2, 512], F32))
            rstd = stack2.enter_context(sbt("rstd", [128, 1, 512], F32))
            return (sq, [Region(f"sq{i}") for i in range(4)], lnv, [Region("lnv0"), Region("lnv1")],
                    rstd, [Region("rstd0")])

        def mm_group(bank, pairs, M=128, N=512, n0=0, reads_extra=()):
            n = len(pairs)
            for i, (l, r, regs) in enumerate(pairs):
                sc.op("pe", lambda e, l=l, r=r, i=i: e.matmul(PS[bank][0:M, n0:n0 + N], l, r, start=(i == 0), stop=(i == n - 1)),
                      reads=list(regs) + list(reads_extra), writes=[PSR[bank]], signal=(i == n - 1))

        xnbuf = [ec(nc.sbuf_tensor(f"xnbuf{b}", [128, 8, 1024], BF16)) for b in range(2)]
        XNR = [[[Region(f"xn{b}_{c}_{t}") for t in range(2)] for c in range(8)] for b in range(2)]
        pstate = {"k": 0, "pre": None, "pre_next": None}
        dve_bg = []

        def dve_tick(n=1):
            for _ in range(n):
                if dve_bg:
                    dve_bg.pop(0)()

        def get_xn(gname, tts, scr):
            b = pstate["k"] % 2
            xn, XN = xnbuf[b], XNR[b]
            while dve_bg:
                dve_bg.pop(0)()
            if pstate["pre"] != (gname, tuple(tts)):
                norm_h(gname, tts, xn, XN, scr)
            return xn, XN

        def schedule_prenorm(nxt):
            pstate["pre_next"] = None
            if nxt is None:
                return
            gname, tts = nxt
            if not all(rstd_valid[tt] for tt in tts):
                return
            b = (pstate["k"] + 1) % 2
            xn, XN = xnbuf[b], XNR[b]
            for tl, tt in enumerate(tts):
                for c in range(8):
                    dve_bg.append(lambda c=c, tl=tl, tt=tt: sc.op("dve", lambda e: e.scalar_tensor_tensor(
                        out=xn[:, c, tl * 512:(tl + 1) * 512], in0=hT[:, c, tt * 512:(tt + 1) * 512],
                        scalar=cvc(gname, c), in1=rstdP[:, tt, :], op0=ALU.mult, op1=ALU.mult),
                        reads=[H[c][tt], RSTDP[tt], R_cv], writes=[XN[c][tl]]))
            pstate["pre_next"] = (gname, tuple(tts))

        def phase_end():
            pstate["pre"] = pstate["pre_next"]
            pstate["k"] += 1
            sc.barrier()

        pool_bg = []

        def table_steps(s):
            Rp, Ra, Rki, Rkf = T_R["tposi"], T_R["tang"], T_R["tki"], T_R["tkf"]
            Ra2 = [T_R["ta2"], T_R["ta2b"]]
            a2s = [ta2, ta2b]
            steps = []
            pending_sin = []
            k = 0
            for piece in range(S // TW):
                tt = (piece * TW) // 512
                cs = slice(piece * TW, (piece + 1) * TW)

                def st_load(cs=cs):
                    sc.dma("sp", "tpos", tposi[:], pos[s:s + 1, cs].partition_broadcast(64), writes=[Rp])
                    sc.op("dve", lambda e: e.tensor_copy(out=tang[:], in_=tposi[:]), reads=[Rp], writes=[Ra])
                steps.append(st_load)
                steps.append(lambda: sc.op("dve", lambda e: e.tensor_scalar(out=tang[:], in0=tang[:], scalar1=cv[0:64, CV["invf"]:CV["invf"] + 1], scalar2=None, op0=ALU.mult),
                                           reads=[Ra, R_cv], writes=[Ra]))
                for (phn, tab, Rt) in (("phc", tabC, R_tabC[tt]), ("phs", tabS, R_tabS[tt])):
                    a2 = a2s[k % 2]
                    R2 = Ra2[k % 2]
                    steps.append(lambda phn=phn, a2=a2, R2=R2: sc.op("dve", lambda e: e.tensor_scalar(out=a2[:], in0=tang[:], scalar1=cv[0:64, CV[phn]:CV[phn] + 1], scalar2=None, op0=ALU.add),
                                                                      reads=[Ra, R_cv], writes=[R2]))
                    steps.append(lambda a2=a2, R2=R2: sc.op("dve", lambda e: e.tensor_scalar(out=tki[:], in0=a2[:], scalar1=float(1.0 / TWO_PI), scalar2=None, op0=ALU.mult),
                                                             reads=[R2], writes=[Rki]))
                    steps.append(lambda: sc.op("dve", lambda e: e.tensor_copy(out=tkf[:], in_=tki[:]), reads=[Rki], writes=[Rkf]))
                    steps.append(lambda: sc.op("dve", lambda e: e.tensor_scalar(out=tkf[:], in0=tkf[:], scalar1=-TWO_PI, scalar2=None, op0=ALU.mult), reads=[Rkf], writes=[Rkf]))
                    steps.append(lambda a2=a2, R2=R2: sc.op("dve", lambda e: e.tensor_tensor(out=a2[:], in0=a2[:], in1=tkf[:], op=ALU.add), reads=[R2, Rkf], writes=[R2]))
                    steps.append(lambda a2=a2, R2=R2: sc.op("dve", lambda e: e.tensor_scalar(out=tkf[:], in0=a2[:], scalar1=float(np.pi), scalar2=-TWO_PI, op0=ALU.is_gt, op1=ALU.mult),
                                                             reads=[R2], writes=[Rkf]))
                    steps.append(lambda a2=a2, R2=R2: sc.op("dve", lambda e: e.tensor_tensor(out=a2[:], in0=a2[:], in1=tkf[:], op=ALU.add), reads=[R2, Rkf], writes=[R2]))
                    steps.append(lambda a2=a2, R2=R2: sc.op("dve", lambda e: e.tensor_scalar(out=a2[:], in0=a2[:], scalar1=float(-np.pi), scalar2=float(np.pi), op0=ALU.max, op1=ALU.min),
                                                             reads=[R2], writes=[R2]))
                    steps.extend(pending_sin)
                    pending_sin = [lambda a2=a2, R2=R2, tab=tab, cs=cs, Rt=Rt: sc.op("act", lambda e: e.activation(out=tab[:, cs], in_=a2[:], func=AF.Sin), reads=[R2], writes=[Rt])]
                    k += 1
            steps.extend(pending_sin)
            return steps

        def flush_pool_bg():
            while pool_bg:
                pool_bg.pop(0)()

        def phase_load(s, hf):
            rstd_valid[2 * hf] = rstd_valid[2 * hf + 1] = False
            for c in range(8):
                sc.dma("sp", f"ldx{hf}", hT[:, c, hf * 1024:(hf + 1) * 1024], xT[s, c * 128:(c + 1) * 128, hf * 1024:(hf + 1) * 1024],
                       writes=[H[c][2 * hf], H[c][2 * hf + 1]])

        def phase_conv(l, hf, nxt=None):
            tts = [2 * hf, 2 * hf + 1]
            with ExitStack() as st2:
                e2 = st2.enter_context
                mT = e2(sbt("mT", [128, 8, 1024], BF16))
                zb = e2(sbt("zb", [128, 2, 2 + 1024], F32))
                csb = e2(sbt("csb", [128, 2, 512], F32))
                acc = e2(sbt("acc", [128, 2, 512], F32))
                cp0 = e2(sbt("cp0", [128, 2, 512], F32))
                cp1 = e2(sbt("cp1", [128, 2, 512], F32))
                CP0 = [Region("cp0_0"), Region("cp0_1")]
                CP1 = [Region("cp1_0"), Region("cp1_1")]
                MT = [[Region(f"mT{c}_{t}") for t in range(2)] for c in range(8)]
                ZB = [[Region(f"zbh{q}"), Region(f"zb{q}_0"), Region(f"zb{q}_1")] for q in range(2)]
                CSB = [Region("csb0"), Region("csb1")]
                ACC = [Region("acc0"), Region("acc1")]
                scr = norm_scratch(st2)
                xn, XN = get_xn(f"conv_norm{l}", tts, scr)
                schedule_prenorm(nxt)
                cw = CV[f"convw{l}"]
                it = 0
                for c in range(8):
                    sb, sg_, sv = w_next("win"), w_next("win"), w_next("win")
                    q = c % 2
                    if hf == 0:
                        sc.op("pool", lambda e, q=q: e.memset(zb[:, q, 0:2], 0.0), writes=[ZB[q][0]])
                    else:
                        sc.op("pool", lambda e, q=q, c=c: e.tensor_copy(out=zb[:, q, 0:2], in_=zc[:, c, :]), reads=[ZC[c]], writes=[ZB[q][0]])
                    for tl, tt in enumerate(tts):
                        b0 = 3 * (it % 2)
                        it += 1
                        for j, slot in enumerate((sb, sg_, sv)):
                            mm_group(b0 + j, [(ring[:, slot, k * 128:(k + 1) * 128], xn[:, k, tl * 512:(tl + 1) * 512], [RING[slot], XN[k][tl]]) for k in range(8)])
                        u = it % 2
                        sc.op("act", lambda e, u=u, b0=b0: e.activation(out=csb[:, u, :], in_=PS[b0 + 1][:], func=AF.Copy),
                              reads=[PSR[b0 + 1]], writes=[CSB[u]])
                        z0 = 2 + tl * 512
                        sc.op("dve", lambda e, u=u, b0=b0, q=q, z0=z0: e.tensor_tensor(out=zb[:, q, z0:z0 + 512], in0=PS[b0 + 2][:], in1=csb[:, u, :], op=ALU.mult),
                              reads=[PSR[b0 + 2], CSB[u]], writes=[ZB[q][1 + tl]])
                        zr = [ZB[q][tl], ZB[q][1 + tl]]
                        sc.op("act", lambda e, u=u, q=q, z0=z0, c=c: e.activation(out=cp1[:, u, :], in_=zb[:, q, z0 - 1:z0 + 511], func=AF.Identity, scale=cv[:, cw + 8 + c:cw + 9 + c]),
                              reads=zr + [R_cv], writes=[CP1[u]])
                        sc.op("act", lambda e, u=u, q=q, z0=z0, c=c: e.activation(out=cp0[:, u, :], in_=zb[:, q, z0 - 2:z0 + 510], func=AF.Identity, scale=cv[:, cw + c:cw + 1 + c]),
                              reads=zr + [R_cv], writes=[CP0[u]])
                        sc.op("dve", lambda e, u=u, q=q, z0=z0, c=c: e.scalar_tensor_tensor(out=acc[:, u, :], in0=zb[:, q, z0:z0 + 512], scalar=cv[:, cw + 16 + c:cw + 17 + c], in1=cp1[:, u, :], op0=ALU.mult, op1=ALU.add),
                              reads=zr + [CP1[u], R_cv], writes=[ACC[u]])
                        sc.op("dve", lambda e, u=u: e.tensor_tensor(out=acc[:, u, :], in0=acc[:, u, :], in1=cp0[:, u, :], op=ALU.add),
                              reads=[ACC[u], CP0[u]], writes=[ACC[u]])
                        sc.op("dve", lambda e, u=u, b0=b0, c=c, tl=tl: e.tensor_tensor(out=mT[:, c, tl * 512:(tl + 1) * 512], in0=PS[b0][:], in1=acc[:, u, :], op=ALU.mult),
                              reads=[PSR[b0], ACC[u]], writes=[MT[c][tl]])
                        dve_tick()
                    if hf == 0:
                        sc.op("pool", lambda e, q=q, c=c: e.tensor_copy(out=zc[:, c, :], in_=zb[:, q, 1024:1026]), reads=[ZB[q][2]], writes=[ZC[c]])
                    for slot in (sb, sg_, sv):
                        w_free(slot)
                it = 0
                pst = PostStats(scr, tts, banks=(4, 5))
                for dc in range(8):
                    so = w_next("wout")
                    for tl, tt in enumerate(tts):
                        bank = 6 + (it % 2)
                        it += 1
                        pst.tick()
                        mm_group(bank, [(ring[:, so, k * 128:(k + 1) * 128], mT[:, k, tl * 512:(tl + 1) * 512], [RING[so], MT[k][tl]]) for k in range(8)])
                        sc.op("dve", lambda e, dc=dc, tt=tt, bank=bank: e.tensor_tensor(out=hT[:, dc, tt * 512:(tt + 1) * 512], in0=hT[:, dc, tt * 512:(tt + 1) * 512], in1=PS[bank][:], op=ALU.add),
                              reads=[H[dc][tt], PSR[bank]], writes=[H[dc][tt]])
                        pst.push(dc, tl)
                    w_free(so)
                pst.finish()
                phase_end()

        def phase_ffn(gname, hf, nxt=None):
            tts = [2 * hf, 2 * hf + 1]
            with ExitStack() as st2:
                e2 = st2.enter_context
                aT = e2(sbt("aT", [128, FG, 1024], BF16))
                sgb = e2(sbt("sgb", [128, 2, 512], F32))
                AT = [[Region(f"aT{c}_{t}") for t in range(2)] for c in range(FG)]
                SG = [Region("sg0"), Region("sg1")]
                scr = norm_scratch(st2)
                xn, XN = get_xn(gname, tts, scr)
                schedule_prenorm(nxt)
                ita = 0
                itb = 0
                for g0 in range(0, NFC, FG):
                    fcs = list(range(g0, min(g0 + FG, NFC)))
                    for i, fc in enumerate(fcs):
                        sg_, su = w_next("w13"), w_next("w13")
                        for tl in range(2):
                            b0 = 2 * (ita % 2)
                            u = ita % 2
                            ita += 1
                            mm_group(b0, [(ring[:, sg_, k * 128:(k + 1) * 128], xn[:, k, tl * 512:(tl + 1) * 512], [RING[sg_], XN[k][tl]]) for k in range(8)])
                            mm_group(b0 + 1, [(ring[:, su, k * 128:(k + 1) * 128], xn[:, k, tl * 512:(tl + 1) * 512], [RING[su], XN[k][tl]]) for k in range(8)])
                            sc.op("act", lambda e, u=u, b0=b0: e.activation(out=sgb[:, u, :], in_=PS[b0][:], func=AF.Silu),
                                  reads=[PSR[b0]], writes=[SG[u]])
                            sc.op("dve", lambda e, u=u, b0=b0, i=i, tl=tl: e.tensor_tensor(out=aT[:, i, tl * 512:(tl + 1) * 512], in0=PS[b0 + 1][:], in1=sgb[:, u, :], op=ALU.mult),
                                  reads=[PSR[b0 + 1], SG[u]], writes=[AT[i][tl]])
                            dve_tick()
                        w_free(sg_)
                        w_free(su)
                    s2 = [w_next("w2") for _ in fcs]
                    lastg = (g0 + FG >= NFC)
                    pst = PostStats(scr, tts, banks=(2, 3), delay=3) if lastg else None
                    for tl, tt in enumerate(tts):
                        for dc in range(8):
                            bank = 4 + (itb % 4)
                            itb += 1
                            if lastg:
                                pst.tick()
                            mm_group(bank, [(ring[:, s2[i], dc * 128:(dc + 1) * 128], aT[:, i, tl * 512:(tl + 1) * 512], [RING[s2[i]], AT[i][tl]]) for i in range(len(fcs))])
                            sc.op("dve", lambda e, dc=dc, tt=tt, bank=bank: e.tensor_tensor(out=hT[:, dc, tt * 512:(tt + 1) * 512], in0=hT[:, dc, tt * 512:(tt + 1) * 512], in1=PS[bank][:], op=ALU.add),
                                  reads=[H[dc][tt], PSR[bank]], writes=[H[dc][tt]])
                            if lastg:
                                pst.push(dc, tl)
                    for s_ in s2:
                        w_free(s_)
                    if lastg:
                        pst.finish()
                phase_end()

        def phase_kv(hf, nxt=None):
            tts = [2 * hf, 2 * hf + 1]
            with ExitStack() as st2:
                e2 = st2.enter_context
                t12 = e2(sbt("t12", [64, 2, 512], F32))
                T12 = [Region("t1"), Region("t2")]
                scr = norm_scratch(st2)
                sq, SQ, lnv, LNV, rstd, RSTD = scr
                xn, XN = get_xn("kv_norm", tts, scr)
                schedule_prenorm(nxt)
                sl0, sl1, sr, srs = w_next("wdkv"), w_next("wdkv"), w_next("wdkvr"), w_next("wdkvr")
                for tl, tt in enumerate(tts):
                    xs = lambda k: xn[:, k, tl * 512:(tl + 1) * 512]
                    mm_group(0, [(ring[:, sl0, k * 128:(k + 1) * 128], xs(k), [RING[sl0], XN[k][tl]]) for k in range(8)])
                    mm_group(1, [(ring[:, sl1, k * 128:(k + 1) * 128], xs(k), [RING[sl1], XN[k][tl]]) for k in range(8)])
                    mm_group(2, [(ring[:, sr, k * 64:(k + 1) * 64], xs(k), [RING[sr], XN[k][tl]]) for k in range(8)], M=64)
                    mm_group(3, [(ring[:, srs, k * 64:(k + 1) * 64], xs(k), [RING[srs], XN[k][tl]]) for k in range(8)], M=64)
                    for kc in range(2):
                        sc.op("act", lambda e, kc=kc: e.activation(out=sq[:, kc, :], in_=PS[kc][:], func=AF.Square), reads=[PSR[kc]], writes=[SQ[kc]])
                    mm_group(4, [(ones[:], sq[:, kc, :], [R_ones, SQ[kc]]) for kc in range(2)])
                    ln_exp_rstd(PS[4][:], PSR[4], rstd[:, 0, :], RSTD[0], lnv[:, 0, :], LNV[0], 1.0 / KVL)
                    for kc in range(2):
                        sc.op("dve", lambda e, kc=kc: e.scalar_tensor_tensor(out=clT[:, kc, tt * 512:(tt + 1) * 512], in0=PS[kc][:], scalar=cvc("lat_norm", kc), in1=rstd[:, 0, :], op0=ALU.mult, op1=ALU.mult),
                              reads=[PSR[kc], RSTD[0], R_cv], writes=[CL[kc][tt]])
                    sc.op("dve", lambda e: e.tensor_tensor(out=t12[:, 0, :], in0=PS[2][0:64, :], in1=tabC[:, tt * 512:(tt + 1) * 512], op=ALU.mult),
                          reads=[PSR[2], R_tabC[tt]], writes=[T12[0]])
                    sc.op("dve", lambda e: e.tensor_tensor(out=t12[:, 1, :], in0=PS[3][0:64, :], in1=tabS[:, tt * 512:(tt + 1) * 512], op=ALU.mult),
                          reads=[PSR[3], R_tabS[tt]], writes=[T12[1]])
                    sc.op("dve", lambda e: e.tensor_tensor(out=kpe[0:64, tt * 512:(tt + 1) * 512], in0=t12[:, 0, :], in1=t12[:, 1, :], op=ALU.add),
                          reads=T12, writes=[KPE[tt]])
                for s_ in (sl0, sl1, sr, srs):
                    w_free(s_)
                phase_end()

        def phase_mla(j, hf, nxt=None):
            tts = [2 * hf, 2 * hf + 1]
            nk4 = 2 * (hf + 1)
            with ExitStack() as st2:
                e2 = st2.enter_context
                cqn = e2(sbt("cqn", [128, 3, 1024], BF16))
                CQN = [[Region(f"cqn{m}_{t}") for t in range(2)] for m in range(3)]
                KT = e2(sbt("KT", [128, 2, S], BF16))
                KTR = [[Region(f"KT{q}_{g}") for g in range(4)] for q in range(2)]
                V = e2(sbt("V", [128, 2, 16, 128], BF16))
                VR = [[Region(f"V{q}_{g}") for g in range(4)] for q in range(2)]
                qn = e2(sbt("qn", [128, 2, 512], BF16))
                QN = [Region("qn0"), Region("qn1")]
                qpe = e2(sbt("qpe", [128, 2, 512], BF16))
                QPE = [Region("qpe0"), Region("qpe1")]
                t12 = e2(sbt("t12", [64, 2, 512], F32))
                T12 = [Region("t1"), Region("t2")]
                Pb = e2(sbt("Pb", [128, 4, 512], BF16))
                PR = [Region(f"P{i}") for i in range(4)]
                rec = e2(sbt("rec", [128, 512], F32))
                REC = Region("rec")
                pdsb = e2(sbt("pdsb", [128, 512], F32))
                PDSB = Region("pdsb")
                scr = norm_scratch(st2)
                sq, SQ, lnv, LNV, rstd, RSTD = scr
                sc.op("pool", lambda e: e.memset(qpe[:], 0.0), writes=QPE)
                xn, XN = get_xn(f"mla_norm{j}", tts, scr)
                attnT, ATT = xn, XN
                schedule_prenorm(nxt)
                sdq = [w_next("wdq") for _ in range(3)]
                for tl, tt in enumerate(tts):
                    for m in range(3):
                        mm_group(m, [(ring[:, sdq[m], k * 128:(k + 1) * 128], xn[:, k, tl * 512:(tl + 1) * 512], [RING[sdq[m]], XN[k][tl]]) for k in range(8)])
                    for m in range(3):
                        q = m % 2
                        sc.op("act", lambda e, m=m, q=q: e.activation(out=sq[:, q, :], in_=PS[m][:], func=AF.Square), reads=[PSR[m]], writes=[SQ[q]])
                        sc.op("pe", lambda e, m=m, q=q: e.matmul(PS[3][:], ones[:], sq[:, q, :], start=(m == 0), stop=(m == 2)),
                              reads=[SQ[q], R_ones], writes=[PSR[3]], signal=True)
                    ln_exp_rstd(PS[3][:], PSR[3], rstd[:, 0, :], RSTD[0], lnv[:, 0, :], LNV[0], 1.0 / QL)
                    for m in range(3):
                        sc.op("dve", lambda e, m=m: e.scalar_tensor_tensor(out=cqn[:, m, tl * 512:(tl + 1) * 512], in0=PS[m][:], scalar=cvc(f"q_norm{j}", m), in1=rstd[:, 0, :], op0=ALU.mult, op1=ALU.mult),
                              reads=[PSR[m], RSTD[0], R_cv], writes=[CQN[m][tl]])
                for s_ in sdq:
                    w_free(s_)

                suq = {}
                bg = []
                mctr = [0]

                def misc_bank():
                    mctr[0] += 1
                    return 6 + (mctr[0] % 2)

                def kvgen_items(h):
                    q = h % 2
                    s_ = w_next("wukv")
                    items = []

                    def kt_item(g):
                        bank = misc_bank()
                        mm_group(bank, [(ring[:, s_, kc * 256:kc * 256 + 128], clT[:, kc, g * 512:(g + 1) * 512], [RING[s_], CL[kc][g]]) for kc in range(2)])
                        sc.op("dve", lambda e: e.tensor_copy(out=KT[:, q, g * 512:(g + 1) * 512], in_=PS[bank][:]),
                              reads=[PSR[bank]], writes=[KTR[q][g]])

                    def v_item(g, last):
                        bank = misc_bank()
                        for i4 in range(4):
                            kt = g * 4 + i4
                            for kc in range(2):
                                lastmm = (i4 == 3 and kc == 1)
                                sc.op("pe", lambda e, kc=kc, kt=kt, i4=i4: e.matmul(
                                    PS[bank][:, i4 * 128:(i4 + 1) * 128], clT[:, kc, kt * 128:(kt + 1) * 128],
                                    ring[:, s_, kc * 256 + 128:kc * 256 + 256], start=(kc == 0), stop=(kc == 1)),
                                    reads=[RING[s_], CL[kc][g]], writes=[PSR[bank]], signal=lastmm)
                        sc.op("dve", lambda e: e.tensor_copy(out=V[:, q, g * 4:(g + 1) * 4, :], in_=PS[bank][:].rearrange("p (a b) -> p a b", a=4)),
                              reads=[PSR[bank]], writes=[VR[q][g]])
                        if last:
                            w_free(s_)

                    for g in range(nk4):
                        items.append(lambda g=g: kt_item(g))
                    for g in range(nk4):
                        items.append(lambda g=g: v_item(g, g == nk4 - 1))
                    return items

                def qproj_items(n):
                    h, tl = n // 2, n % 2
                    tt = tts[tl]
                    u = n % 2
                    if tl == 0:
                        suq[h] = w_next("wuq")
                    s_ = suq[h]
                    cq = lambda m: cqn[:, m, tl * 512:(tl + 1) * 512]

                    def a():
                        bank = misc_bank()
                        mm_group(bank, [(ring[:, s_, m * 256:m * 256 + 128], cq(m), [RING[s_], CQN[m][tl]]) for m in range(3)])
                        sc.op("dve", lambda e: e.tensor_copy(out=qn[:, u, :], in_=PS[bank][:]), reads=[PSR[bank]], writes=[QN[u]])
                        if tl == 1:
                            w_free(s_)

                    def b():
                        bank = misc_bank()
                        mm_group(bank, [(ring[:, s_, m * 256 + 128:m * 256 + 192], cq(m), [RING[s_], CQN[m][tl]]) for m in range(3)], M=64)
                        sc.op("dve", lambda e: e.tensor_tensor(out=t12[:, 0, :], in0=PS[bank][0:64, :], in1=tabC[:, tt * 512:(tt + 1) * 512], op=ALU.mult),
                              reads=[PSR[bank], R_tabC[tt]], writes=[T12[0]])

                    def c():
                        bank = misc_bank()
                        mm_group(bank, [(ring[:, s_, m * 256 + 192:m * 256 + 256], cq(m), [RING[s_], CQN[m][tl]]) for m in range(3)], M=64)
                        sc.op("dve", lambda e: e.tensor_tensor(out=t12[:, 1, :], in0=PS[bank][0:64, :], in1=tabS[:, tt * 512:(tt + 1) * 512], op=ALU.mult),
                              reads=[PSR[bank], R_tabS[tt]], writes=[T12[1]])
                        sc.op("dve", lambda e: e.tensor_tensor(out=qpe[0:64, u, :], in0=t12[:, 0, :], in1=t12[:, 1, :], op=ALU.add),
                              reads=T12, writes=[QPE[u]])

                    return [b, c, a]

                def bg_step():
                    if bg:
                        bg.pop(0)()

                def attention(n):
                    h, tl = n // 2, n % 2
                    tt = tts[tl]
                    u = n % 2
                    q = h % 2
                    bo, bd = 3 + (n % 2), 5
                    nkt = 4 * (tt + 1)

                    def s_mm(i):
                        d = i - 4 * tt
                        q0 = 128 * d if d > 0 else 0
                        sb = i % 3
                        g = i // 4
                        sc.op("pe", lambda e: e.matmul(PS[sb][:, q0:512], KT[:, q, i * 128:(i + 1) * 128], qn[:, u, q0:512], start=True, stop=False),
                              reads=[KTR[q][g], QN[u]], writes=[PSR[sb]], signal=False)
                        sc.op("pe", lambda e: e.matmul(PS[sb][:, q0:512], kpe[:, i * 128:(i + 1) * 128], qpe[:, u, q0:512], start=False, stop=(d < 0)),
                              reads=[KPE[g], QPE[u]], writes=[PSR[sb]], signal=(d < 0))
                        if d >= 0:
                            sc.op("pe", lambda e: e.matmul(PS[sb][:, q0:q0 + 128], ident[:], mneg[:], start=False, stop=True),
                                  reads=[R_id], writes=[PSR[sb]], signal=True)

                    def exp_p(i):
                        d = i - 4 * tt
                        q0 = 128 * d if d > 0 else 0
                        sb = i % 3
                        pi = i % 4
                        sc.op("act", lambda e: e.activation(out=Pb[:, pi, q0:512], in_=PS[sb][:, q0:512], func=AF.Exp, scale=SCALE),
                              reads=[PSR[sb]], writes=[PR[pi]])

                    def pv_mm(i):
                        d = i - 4 * tt
                        q0 = 128 * d if d > 0 else 0
                        sb = i % 3
                        pi = i % 4
                        g = i // 4
                        sc.op("pe", lambda e: e.matmul(PS[bo][:, q0:512], V[:, q, i, :], Pb[:, pi, q0:512], start=(i == 0), stop=(i == nkt - 1)),
                              reads=[VR[q][g], PR[pi]], writes=[PSR[bo]], signal=False)
                        sc.op("pe", lambda e: e.matmul(PS[bd][:, q0:512], ones[:], Pb[:, pi, q0:512], start=(i == 0), stop=(i == nkt - 1)),
                              reads=[R_ones, PR[pi]], writes=[PSR[bd]], signal=True)

                    def prologue():
                        s_mm(0)
                        if nkt > 1:
                            s_mm(1)
                        exp_p(0)

                    def body():
                        for i in range(nkt):
                            if i + 2 < nkt:
                                s_mm(i + 2)
                            if i + 1 < nkt:
                                exp_p(i + 1)
                            pv_mm(i)
                            bg_step()
                            if nkt <= 4:
                                bg_step()
                            dve_tick()

                    def finalize():
                        sc.op("act", lambda e: e.activation(out=pdsb[:], in_=PS[bd][:], func=AF.Ln), reads=[PSR[bd]], writes=[PDSB])
                        sc.op("act", lambda e: e.activation(out=rec[:], in_=pdsb[:], func=AF.Exp, scale=-1.0), reads=[PDSB], writes=[REC])
                        sc.op("dve", lambda e: e.tensor_tensor(out=attnT[:, h, tl * 512:(tl + 1) * 512], in0=PS[bo][:], in1=rec[:], op=ALU.mult),
                              reads=[PSR[bo], REC], writes=[ATT[h][tl]])

                    return prologue, body, finalize

                for it_ in kvgen_items(0):
                    it_()
                for it_ in qproj_items(0):
                    it_()
                parts = attention(0)
                parts[0]()
                for n in range(16):
                    h, tl = n // 2, n % 2
                    if tl == 0 and h + 1 < NH:
                        bg.extend(kvgen_items(h + 1))
                    if n + 1 < 16:
                        qitems = qproj_items(n + 1)
                    else:
                        qitems = []
                    if tl == 1:
                        bg.extend(qitems)
                        parts[1]()
                        while bg:
                            bg_step()
                    else:
                        bg[0:0] = qitems
                        parts[1]()
                        while any(it_ in bg for it_ in qitems):
                            bg_step()
                    nparts = attention(n + 1) if n + 1 < 16 else None
                    if nparts is not None:
                        nparts[0]()
                    parts[2]()
                    parts = nparts
                it = 0
                pst = PostStats(scr, tts, banks=(4, 5))
                for dc in range(8):
                    so = w_next("wo")
                    for tl, tt in enumerate(tts):
                        bank = 6 + (it % 2)
                        it += 1
                        pst.tick()
                        mm_group(bank, [(ring[:, so, k * 128:(k + 1) * 128], attnT[:, k, tl * 512:(tl + 1) * 512], [RING[so], ATT[k][tl]]) for k in range(8)])
                        sc.op("dve", lambda e, dc=dc, tt=tt, bank=bank: e.tensor_tensor(out=hT[:, dc, tt * 512:(tt + 1) * 512], in0=hT[:, dc, tt * 512:(tt + 1) * 512], in1=PS[bank][:], op=ALU.add),
                              reads=[H[dc][tt], PSR[bank]], writes=[H[dc][tt]])
                        pst.push(dc, tl)
                    w_free(so)
                pst.finish()
                phase_end()

        def phase_final(s, hf):
            tts = [2 * hf, 2 * hf + 1]
            with ExitStack() as st2:
                scr = norm_scratch(st2)
                yst = st2.enter_context(sbt("yst", [128, 8, 1024], F32))
                YST = [Region(f"yst{c}") for c in range(8)]
                for tl, tt in enumerate(tts):
                    if not rstd_valid[tt]:
                        stats_from_h(tt, scr, 6 + (tl % 2))
                for c in range(8):
                    for tl, tt in enumerate(tts):
                        sc.op("dve", lambda e, c=c, tl=tl, tt=tt: e.scalar_tensor_tensor(
                            out=yst[:, c, tl * 512:(tl + 1) * 512], in0=hT[:, c, tt * 512:(tt + 1) * 512],
                            scalar=cvc("final_norm", c), in1=rstdP[:, tt, :], op0=ALU.mult, op1=ALU.mult),
                            reads=[H[c][tt], RSTDP[tt], R_cv], writes=[YST[c]])
                    sc.dma("sp", "st", yT[s, c * 128:(c + 1) * 128, hf * 1024:(hf + 1) * 1024], yst[:, c, :], reads=[YST[c]])
                if s + 1 < n_seq:
                    phase_load(s + 1, hf)
                d_ = sc.dsem["st"]
                for e_ in ("sp", "dve", "act", "pool"):
                    sc._wait(e_, "st", d_[1])
                sc.barrier()

        phases = []
        for l in range(2):
            for hf in range(2):
                phases.append(("conv", l, hf))
            for hf in range(2):
                phases.append(("ffn", f"cffn_norm{l}", hf))
        for hf in range(2):
            phases.append(("kv", hf))
        for j in range(2):
            for hf in range(2):
                phases.append(("mla", j, hf))
            for hf in range(2):
                phases.append(("ffn", f"mffn_norm{j}", hf))
        if n_phases is not None:
            phases = phases[:n_phases]
        full = (n_phases is None)
        if not full and n_seq > 1:
            raise NotImplementedError("truncated program supports n_seq=1 only")
        wstate["cap"] = NT * n_seq if full else _tiles_for_phases(phases)
        w_pump()

        def norm_info(ph):
            hf = ph[-1]
            tts = [2 * hf, 2 * hf + 1]
            if ph[0] == "conv":
                return (f"conv_norm{ph[1]}", tts)
            if ph[0] == "ffn":
                return (ph[1], tts)
            if ph[0] == "kv":
                return ("kv_norm", tts)
            return (f"mla_norm{ph[1]}", tts)

        def run_phase(ph, nxt):
            nxt = norm_info(nxt) if nxt is not None else None
            if ph[0] == "conv":
                phase_conv(ph[1], ph[2], nxt)
            elif ph[0] == "ffn":
                phase_ffn(ph[1], ph[2], nxt)
            elif ph[0] == "kv":
                phase_kv(ph[1], nxt)
            elif ph[0] == "mla":
                phase_mla(ph[1], ph[2], nxt)

        phase_load(0, 0)
        phase_load(0, 1)
        pool_bg.extend(table_steps(0))
        last_mla = max([i for i, ph in enumerate(phases) if ph[0] == "mla"], default=None)
        for s in range(n_seq):
            for ip, ph in enumerate(phases):
                if ph[0] == "kv":
                    flush_pool_bg()
                run_phase(ph, phases[ip + 1] if ip + 1 < len(phases) else None)
                if s + 1 < n_seq and ip == last_mla:
                    pool_bg.extend(table_steps(s + 1))
            flush_pool_bg()
            for hf in range(2):
                phase_final(s, hf)
        d = sc.dsem["st"]
        nc.sync.wait_ge(d[0], d[1])
        build_program.last_nins = sc.nins
    return nc, plan


def _tiles_for_phases(phases):
    n = 0
    for ph in phases:
        if ph[0] == "conv":
            n += 24 + 8
        elif ph[0] == "ffn":
            n += 66
        elif ph[0] == "kv":
            n += 4
        elif ph[0] == "mla":
            n += 3 + 16 + 8
    return n


def _weight_plan():
    plan = []

    def ffn(w13, w2, l):
        for hf in range(2):
            for g0 in range(0, NFC, FG):
                fcs = list(range(g0, min(g0 + FG, NFC)))
                for fc in fcs:
                    plan.append(dict(kind="w13", w=w13, l=l, K=D, cols=np.arange(fc * 128, fc * 128 + 128), L=1024))
                    plan.append(dict(kind="w13", w=w13, l=l, K=D, cols=np.arange(DFF + fc * 128, DFF + fc * 128 + 128), L=1024))
                for fc in fcs:
                    plan.append(dict(kind="w2", w=w2, l=l, rows=(fc * 128, fc * 128 + 128), L=1024))

    for l in range(2):
        for hf in range(2):
            for c in range(8):
                for off in (0, D, 2 * D):
                    plan.append(dict(kind="win", w="conv_w_in", l=l, K=D, cols=np.arange(off + c * 128, off + c * 128 + 128), L=1024))
            for dc in range(8):
                plan.append(dict(kind="wout", w="conv_w_out", l=l, K=D, cols=np.arange(dc * 128, dc * 128 + 128), L=1024))
        ffn("conv_ffn_w13", "conv_ffn_w2", l)
    for hf in range(2):
        plan.append(dict(kind="wdkv", w="w_dkv", l=None, K=D, cols=np.arange(0, 128), L=1024))
        plan.append(dict(kind="wdkv", w="w_dkv", l=None, K=D, cols=np.arange(128, 256), L=1024))
        plan.append(dict(kind="wdkvr", w="w_dkv", l=None, K=D, cols=np.arange(256, 320), L=512))
        plan.append(dict(kind="wdkvr", w="w_dkv", l=None, K=D, cols=np.concatenate([np.arange(288, 320), np.arange(256, 288)]), L=512))
    for j in range(2):
        for hf in range(2):
            for m in range(3):
                plan.append(dict(kind="wdq", w="mla_w_dq", l=j, K=D, cols=np.arange(m * 128, m * 128 + 128), L=1024))
            order = [("wukv", 0), ("wuq", 0)]
            for n in range(16):
                h, tl = n // 2, n % 2
                if tl == 0 and h + 1 < NH:
                    order.append(("wukv", h + 1))
                if n + 1 < 16 and (n + 1) % 2 == 0:
                    order.append(("wuq", (n + 1) // 2))
            for kind, h in order:
                if kind == "wukv":
                    plan.append(dict(kind="wukv", w="w_ukv", l=None, K=KVL, cols=np.arange(h * 256, h * 256 + 256), L=512))
                else:
                    b = h * 192
                    cols = np.concatenate([np.arange(b, b + 128), np.arange(b + 128, b + 192), np.arange(b + 160, b + 192), np.arange(b + 128, b + 160)])
                    plan.append(dict(kind="wuq", w="mla_w_uq", l=j, K=QL, cols=cols, L=768))
            for dc in range(8):
                plan.append(dict(kind="wo", w="mla_w_o", l=j, K=D, cols=np.arange(dc * 128, dc * 128 + 128), L=1024))
        ffn("mla_ffn_w13", "mla_ffn_w2", j)
    return plan


def _dedupe(plan):
    seen = {}
    for sp in plan:
        key = (sp["w"], sp["l"], tuple(sp["rows"]) if "rows" in sp else tuple(int(c) for c in sp["cols"]))
        if key not in seen:
            seen[key] = len(seen)
        sp["u"] = seen[key]
    return len(seen)


def _make_wst(plan, inputs):
    NU = _dedupe(plan)
    wst = np.zeros((NU, 128, 1024), np.float32)
    done = set()
    for sp in plan:
        i = sp["u"]
        if i in done:
            continue
        done.add(i)
        W = inputs[sp["w"]]
        if sp["l"] is not None:
            W = W[sp["l"]]
        if "rows" in sp:
            r0, r1 = sp["rows"]
            wst[i, :, :] = W[r0:r1, :]
        else:
            K = sp["K"]
            cols = sp["cols"]
            kc = K // 128
            t = W[:, cols].reshape(kc, 128, len(cols)).transpose(1, 0, 2).reshape(128, kc * len(cols))
            wst[i, :, :t.shape[1]] = t
    return wst


def _make_cvec(inputs):
    cv = np.zeros((128, NCV), np.float32)

    def put(name, vec):
        n = vec.shape[0] // 128
        cv[:, CV[name]:CV[name] + n] = np.asarray(vec, np.float32).reshape(n, 128).T

    for l in range(2):
        put(f"conv_norm{l}", inputs["conv_norm_g"][l])
        put(f"cffn_norm{l}", inputs["conv_ffn_norm_g"][l])
        put(f"mla_norm{l}", inputs["mla_norm_g"][l])
        put(f"mffn_norm{l}", inputs["mla_ffn_norm_g"][l])
        put(f"q_norm{l}", inputs["mla_q_norm_g"][l])
        for k in range(3):
            cv[:, CV[f"convw{l}"] + 8 * k:CV[f"convw{l}"] + 8 * k + 8] = np.asarray(inputs["conv_w"][l][k], np.float32).reshape(8, 128).T
    put("kv_norm", inputs["kv_norm_g"])
    put("final_norm", inputs["final_norm_g"])
    put("lat_norm", inputs["kv_latent_norm_g"])
    invf = (np.float32(10000.0) ** (-np.arange(0, ROPE, 2, dtype=np.float32) / np.float32(ROPE))).astype(np.float32)
    p = np.arange(128)
    cv[:, CV["invf"]] = invf[p % 32]
    cv[:, CV["phc"]] = np.float32(np.pi / 2)
    cv[:, CV["phs"]] = np.where((p % 64) < 32, np.float32(np.pi), np.float32(0.0))
    cv[:, CV["eps"]] = np.float32(EPS)
    return cv


_PROG_CACHE = {}


def _get_program(n_phases=None, n_seq=NSEQ):
    key = (n_phases, n_seq)
    if key not in _PROG_CACHE:
        _PROG_CACHE[key] = build_program(n_phases, n_seq)
    return _PROG_CACHE[key]


def kernel(**inputs):
    n_phases = inputs.pop("_n_phases", None)
    n_seq = inputs.pop("_n_seq", NSEQ)
    inputs = {k: np.asarray(v) for k, v in inputs.items()}
    nc, plan = _get_program(n_phases, n_seq)
    wst = _make_wst(plan, inputs)
    cvec = _make_cvec(inputs)
    x = inputs["x"]
    posn = inputs["positions"].astype(np.int32)
    n_cores = 8
    in_maps = []
    for c in range(n_cores):
        xs = np.ascontiguousarray(x[NSEQ * c:NSEQ * (c + 1)].transpose(0, 2, 1))
        in_maps.append({"xT": xs, "pos": np.ascontiguousarray(posn[NSEQ * c:NSEQ * (c + 1)]), "cvec": cvec, "wst": wst})
    res = run_bass_kernel_spmd(nc, in_maps, core_ids=list(range(n_cores)))
    out = np.empty_like(x)
    for c in range(n_cores):
        out[NSEQ * c:NSEQ * (c + 1)] = res.results[c]["yT"].transpose(0, 2, 1)
    return out
```
